# Optimizing a Trainium2 kernel written in Bass

```python
import jax
import jax.numpy as jnp
from jax import lax
import numpy as np

D_MODEL = 1024
BATCH = 16
SEQ = 2048
DEPTH = 2

GRID_W = 64
CTX_LEN = 256
CONV_DIM = 512
CONV_K = 31
ML_HEADS = 4
ML_HEAD_DIM = 128
ML_DIM = ML_HEADS * ML_HEAD_DIM
ML_CHUNK = 128
MLA_HEADS = 8
QK_NOPE = 64
QK_ROPE = 32
V_HEAD = 64
Q_LORA = 768
KV_LORA = 256
ROPE_THETA = 10000.0
ROPE_FREQ = QK_ROPE // 4
MLA_SCALE = (QK_NOPE + QK_ROPE) ** -0.5
Q_BLOCK = 128
D_FF = 4 * D_MODEL
N_BRANCH = 3
LN_EPS = 1e-5

COLS = (2 * CONV_DIM, 3 * ML_DIM, ML_DIM, 4 * ML_HEADS, Q_LORA, KV_LORA, QK_ROPE, N_BRANCH * D_MODEL)
IN_SPLITS = tuple(int(s) for s in np.cumsum(COLS)[:-1])
N_IN = int(sum(COLS))
OFF_GIF = COLS[0] + COLS[1] + COLS[2]

kernel_name = 'hybrid_conv_mlstm_mla_block'


def layer_norm(x, g, b):
    xf = x.astype(jnp.float32)
    mu = jnp.mean(xf, -1, keepdims=True)
    var = jnp.mean(jnp.square(xf - mu), -1, keepdims=True)
    return ((xf - mu) * lax.rsqrt(var + LN_EPS) * g + b).astype(x.dtype)


def rms_norm(x, g):
    xf = x.astype(jnp.float32)
    return (xf * lax.rsqrt(jnp.mean(jnp.square(xf), -1, keepdims=True) + LN_EPS) * g).astype(x.dtype)


def axial_rope_tables(n_tokens):
    rows = n_tokens // GRID_W
    rr, cc = jnp.meshgrid(jnp.arange(rows, dtype=jnp.float32), jnp.arange(GRID_W, dtype=jnp.float32), indexing='ij')
    inv = ROPE_THETA ** (-jnp.arange(ROPE_FREQ, dtype=jnp.float32) / ROPE_FREQ)
    ang = jnp.stack([rr.reshape(-1), cc.reshape(-1)], -1)[..., None] * inv
    return jnp.cos(ang), jnp.sin(ang)


def apply_axial_rope(x, cos, sin):
    shp = x.shape
    xf = x.astype(jnp.float32).reshape(shp[:-1] + (2, 2, ROPE_FREQ))
    x1, x2 = xf[..., 0, :], xf[..., 1, :]
    out = jnp.stack([x1 * cos - x2 * sin, x2 * cos + x1 * sin], -2)
    return out.reshape(shp).astype(x.dtype)


def conformer_conv(a, w_dw, b_dw, g, b, w_pw):
    val, gate = jnp.split(a, 2, axis=-1)
    h = val * jax.nn.sigmoid(gate)
    h = lax.conv_general_dilated(h, w_dw[:, None, :].astype(h.dtype), (1,), ((CONV_K // 2, CONV_K // 2),),
                                 dimension_numbers=('NWC', 'WIO', 'NWC'), feature_group_count=CONV_DIM) + b_dw
    return jax.nn.silu(layer_norm(h, g, b)) @ w_pw


def mlstm_inputs(qkv, g_if):
    bsz, t, _ = qkv.shape
    def heads(z):
        return z.reshape(bsz, t, ML_HEADS, ML_HEAD_DIM).transpose(0, 2, 1, 3)
    q, k, v = jnp.split(qkv, 3, axis=-1)
    g = g_if.astype(jnp.float32).reshape(bsz, t, 4, ML_HEADS).transpose(2, 0, 3, 1)
    fwd = (g[0], jax.nn.log_sigmoid(g[1]))
    bwd = (g[2], jax.nn.log_sigmoid(g[3]))
    return heads(q), heads(k) * ML_HEAD_DIM ** -0.5, heads(v), fwd, bwd


def mlstm_init_state(bsz):
    return (jnp.zeros((bsz, ML_HEADS, ML_HEAD_DIM, ML_HEAD_DIM), jnp.float32),
            jnp.zeros((bsz, ML_HEADS, ML_HEAD_DIM), jnp.float32),
            jnp.zeros((bsz, ML_HEADS), jnp.float32))


def mlstm_chunkwise(q, k, v, logi, logf, state):
    bsz, nh, t, dh = q.shape
    nc = t // ML_CHUNK
    def to_chunks(z):
        return jnp.moveaxis(z.reshape(z.shape[:2] + (nc, ML_CHUNK) + z.shape[3:]), 2, 0)
    causal = jnp.tril(jnp.ones((ML_CHUNK, ML_CHUNK), bool))

    def step(carry, xs):
        cmat, nvec, m = carry
        qb, kb, vb, li, lf = xs
        qf, kf, vf = qb.astype(jnp.float32), kb.astype(jnp.float32), vb.astype(jnp.float32)
        bcum = jnp.cumsum(lf, -1)
        log_d = jnp.where(causal, bcum[..., :, None] - bcum[..., None, :] + li[..., None, :], -jnp.inf)
        inter = bcum + m[..., None]
        m_row = jnp.maximum(inter, jnp.max(log_d, -1))
        a_inter = jnp.exp(inter - m_row)
        s = jnp.einsum('bhld,bhsd->bhls', qf, kf) * jnp.exp(log_d - m_row[..., None])
        num = a_inter[..., None] * jnp.einsum('bhld,bhde->bhle', qf, cmat) + jnp.einsum('bhls,bhse->bhle', s, vf)
        den = a_inter * jnp.einsum('bhld,bhd->bhl', qf, nvec) + jnp.sum(s, -1)
        h = num / jnp.maximum(jnp.abs(den), jnp.exp(-m_row))[..., None]
        b_last = bcum[..., -1]
        w_src = b_last[..., None] - bcum + li
        m_new = jnp.maximum(b_last + m, jnp.max(w_src, -1))
        a_state = jnp.exp(b_last + m - m_new)
        w = jnp.exp(w_src - m_new[..., None])
        c_new = a_state[..., None, None] * cmat + jnp.einsum('bhs,bhsd,bhse->bhde', w, kf, vf)
        n_new = a_state[..., None] * nvec + jnp.einsum('bhs,bhsd->bhd', w, kf)
        return (c_new, n_new, m_new), h

    state, hs = lax.scan(step, state, (to_chunks(q), to_chunks(k), to_chunks(v), to_chunks(logi), to_chunks(logf)))
    return jnp.moveaxis(hs, 0, 2).reshape(bsz, nh, t, dh), state


def flip_t(z):
    return jnp.flip(z, axis=2)


def mlstm_bidir(q, k, v, fwd, bwd, st_f, st_b):
    h_f, st_f = mlstm_chunkwise(q, k, v, fwd[0], fwd[1], st_f)
    h_b, st_b = mlstm_chunkwise(flip_t(q), flip_t(k), flip_t(v), flip_t(bwd[0]), flip_t(bwd[1]), st_b)
    return h_f + flip_t(h_b), st_f, st_b


def mlstm_out(h, o_pre, g, w):
    bsz, nh, t, dh = h.shape
    h = h.transpose(0, 2, 1, 3)
    mu = jnp.mean(h, -1, keepdims=True)
    var = jnp.mean(jnp.square(h - mu), -1, keepdims=True)
    h = ((h - mu) * lax.rsqrt(var + LN_EPS)).reshape(bsz, t, ML_DIM) * g
    return (jax.nn.sigmoid(o_pre) * h.astype(o_pre.dtype)) @ w


def mla_project(cq, ckv, kr, qn_g, w_uq, kvn_g, w_ukv, rope):
    bsz, t, _ = cq.shape
    q = (rms_norm(cq, qn_g) @ w_uq).reshape(bsz, t, MLA_HEADS, QK_NOPE + QK_ROPE)
    kv = (rms_norm(ckv, kvn_g) @ w_ukv).reshape(bsz, t, MLA_HEADS, QK_NOPE + V_HEAD)
    qn, qr = q[..., :QK_NOPE], q[..., QK_NOPE:]
    kn, v = kv[..., :QK_NOPE], kv[..., QK_NOPE:]
    if rope is not None:
        cos, sin = rope
        qr = apply_axial_rope(qr, cos[:, None], sin[:, None])
        kr = apply_axial_rope(kr, cos, sin)
    return qn, qr, kn, kr, v


def mla_attend(qn, qr, kn, kr, v):
    s = jnp.einsum('bqhd,bkhd->bhqk', qn, kn) + jnp.einsum('bqhr,bkr->bhqk', qr, kr)
    p = jax.nn.softmax(s.astype(jnp.float32) * MLA_SCALE, axis=-1).astype(v.dtype)
    return jnp.einsum('bhqk,bkhd->bqhd', p, v)


def mla_blockwise(qn, qr, kn, kr, v):
    bsz, t = qn.shape[:2]
    nb = t // Q_BLOCK
    def blocks(z):
        return jnp.moveaxis(z.reshape((bsz, nb, Q_BLOCK) + z.shape[2:]), 1, 0)
    o = lax.map(lambda qb: mla_attend(qb[0], qb[1], kn, kr, v), (blocks(qn), blocks(qr)))
    return jnp.moveaxis(o, 0, 1).reshape(bsz, t, MLA_HEADS * V_HEAD)


def gated_merge(gates, y_a, y_b, y_c, w_out, b_out):
    g_a, g_b, g_c = jnp.split(jax.nn.sigmoid(gates), N_BRANCH, axis=-1)
    return (g_a * y_a + g_b * y_b + g_c * y_c) @ w_out + b_out


def token_mixer(u, uc, rope, w_in, b_in, w_dw, b_dw, cn_g, cn_b, w_conv_out, ml_g, w_ml_out,
                qn_g, w_uq, kvn_g, w_ukv, w_mla_out, w_out, b_out, with_ctx):
    a, qkv, o_pre, g_if, cq, ckv, kr, gates = jnp.split(u @ w_in + b_in, IN_SPLITS, axis=-1)
    a_c, qkv_c, o_pre_c, g_if_c, cq_c, ckv_c, kr_c, gates_c = jnp.split(uc @ w_in + b_in, IN_SPLITS, axis=-1)
    bsz = u.shape[0]
    y_conv = conformer_conv(a, w_dw, b_dw, cn_g, cn_b, w_conv_out)
    q_c, k_c, v_c, fwd_c, bwd_c = mlstm_inputs(qkv_c, g_if_c)
    q_x, k_x, v_x, fwd_x, bwd_x = mlstm_inputs(qkv, g_if)
    s0 = mlstm_init_state(bsz)
    h_c, st_f, st_b = mlstm_bidir(q_c, k_c, v_c, fwd_c, bwd_c, s0, s0)
    h_x, _, _ = mlstm_bidir(q_x, k_x, v_x, fwd_x, bwd_x, st_f, st_b)
    y_ml = mlstm_out(h_x, o_pre, ml_g, w_ml_out)
    qn_c, qr_c, kn_c, kr_cc, vv_c = mla_project(cq_c, ckv_c, kr_c, qn_g, w_uq, kvn_g, w_ukv, None)
    qn_x, qr_x, kn_x, kr_x, vv_x = mla_project(cq, ckv, kr, qn_g, w_uq, kvn_g, w_ukv, rope)
    o_x = mla_blockwise(qn_x, qr_x, jnp.concatenate([kn_c, kn_x], 1), jnp.concatenate([kr_cc, kr_x], 1),
                        jnp.concatenate([vv_c, vv_x], 1))
    y = gated_merge(gates, y_conv, y_ml, o_x @ w_mla_out, w_out, b_out)
    if not with_ctx:
        return y, None
    y_conv_c = conformer_conv(a_c, w_dw, b_dw, cn_g, cn_b, w_conv_out)
    y_ml_c = mlstm_out(h_c, o_pre_c, ml_g, w_ml_out)
    o_c = mla_attend(qn_c, qr_c, kn_c, kr_cc, vv_c).reshape(bsz, uc.shape[1], MLA_HEADS * V_HEAD)
    yc = gated_merge(gates_c, y_conv_c, y_ml_c, o_c @ w_mla_out, w_out, b_out)
    return y, yc


def sq_relu_mlp(u, w1, b1, w2, b2):
    return jnp.square(jax.nn.relu(u @ w1 + b1)) @ w2 + b2


def setup_inputs(seed: int = 0) -> dict:
    key = jax.random.key(seed)
    ks = iter(jax.random.split(key, 40))
    def nrm(shape, std):
        return std * jax.random.normal(next(ks), shape, jnp.float32)
    L, D = DEPTH, D_MODEL
    beta = (8 * DEPTH) ** -0.25
    fb = jnp.linspace(3.0, 6.0, ML_HEADS, dtype=jnp.float32)
    b_in = nrm((L, N_IN), 0.01)
    b_in = b_in.at[:, OFF_GIF + ML_HEADS:OFF_GIF + 2 * ML_HEADS].add(fb)
    b_in = b_in.at[:, OFF_GIF + 3 * ML_HEADS:OFF_GIF + 4 * ML_HEADS].add(fb)
    mla_v = MLA_HEADS * V_HEAD
    return {
        'x': nrm((BATCH, SEQ, D), 1.0),
        'c': nrm((BATCH, D), 1.0),
        'ctx': nrm((BATCH, CTX_LEN, D), 1.0),
        'c_ctx': nrm((D,), 1.0),
        'w_mod': nrm((L, D, 6 * D), 0.5 * D ** -0.5),
        'b_mod': nrm((L, 6 * D), 0.01),
        'w_in': nrm((L, D, N_IN), D ** -0.5),
        'b_in': b_in,
        'w_dw': nrm((L, CONV_K, CONV_DIM), CONV_K ** -0.5),
        'b_dw': nrm((L, CONV_DIM), 0.01),
        'conv_norm_g': 1.0 + nrm((L, CONV_DIM), 0.02),
        'conv_norm_b': nrm((L, CONV_DIM), 0.01),
        'w_conv_out': nrm((L, CONV_DIM, D), beta * CONV_DIM ** -0.5),
        'mlstm_norm_g': 1.0 + nrm((L, ML_DIM), 0.02),
        'w_mlstm_out': nrm((L, ML_DIM, D), beta * ML_DIM ** -0.5),
        'q_norm_g': 1.0 + nrm((L, Q_LORA), 0.02),
        'w_uq': nrm((L, Q_LORA, MLA_HEADS * (QK_NOPE + QK_ROPE)), Q_LORA ** -0.5),
        'kv_norm_g': 1.0 + nrm((L, KV_LORA), 0.02),
        'w_ukv': nrm((L, KV_LORA, MLA_HEADS * (QK_NOPE + V_HEAD)), KV_LORA ** -0.5),
        'w_mla_out': nrm((L, mla_v, D), beta * mla_v ** -0.5),
        'w_out': nrm((L, D, D), beta * D ** -0.5),
        'b_out': nrm((L, D), 0.01),
        'ln1_g': 1.0 + nrm((L, D), 0.02),
        'ln1_b': nrm((L, D), 0.01),
        'w1': nrm((L, D, D_FF), D ** -0.5),
        'b1': nrm((L, D_FF), 0.01),
        'w2': nrm((L, D_FF, D), beta * D_FF ** -0.5),
        'b2': nrm((L, D), 0.01),
        'ln2_g': 1.0 + nrm((L, D), 0.02),
        'ln2_b': nrm((L, D), 0.01),
    }


def reference(x, c, ctx, c_ctx, w_mod, b_mod, w_in, b_in, w_dw, b_dw, conv_norm_g, conv_norm_b, w_conv_out,
              mlstm_norm_g, w_mlstm_out, q_norm_g, w_uq, kv_norm_g, w_ukv, w_mla_out, w_out, b_out,
              ln1_g, ln1_b, w1, b1, w2, b2, ln2_g, ln2_b):
    alpha = (2 * DEPTH) ** 0.25
    rope = axial_rope_tables(x.shape[1])
    for l in range(DEPTH):
        with_ctx = l < DEPTH - 1
        sh1, sc1, g1, sh2, sc2, g2 = jnp.split((jax.nn.silu(c) @ w_mod[l] + b_mod[l])[:, None, :], 6, axis=-1)
        csh1, csc1, cg1, csh2, csc2, cg2 = jnp.split(jax.nn.silu(c_ctx) @ w_mod[l] + b_mod[l], 6, axis=-1)
        u = x * (1.0 + sc1) + sh1
        uc = ctx * (1.0 + csc1) + csh1
        y, yc = token_mixer(u, uc, rope, w_in[l], b_in[l], w_dw[l], b_dw[l], conv_norm_g[l], conv_norm_b[l],
                            w_conv_out[l], mlstm_norm_g[l], w_mlstm_out[l], q_norm_g[l], w_uq[l], kv_norm_g[l],
                            w_ukv[l], w_mla_out[l], w_out[l], b_out[l], with_ctx)
        x = layer_norm(alpha * x + g1 * y, ln1_g[l], ln1_b[l])
        x = layer_norm(alpha * x + g2 * sq_relu_mlp(x * (1.0 + sc2) + sh2, w1[l], b1[l], w2[l], b2[l]), ln2_g[l], ln2_b[l])
        if with_ctx:
            ctx = layer_norm(alpha * ctx + cg1 * yc, ln1_g[l], ln1_b[l])
            ctx = layer_norm(alpha * ctx + cg2 * sq_relu_mlp(ctx * (1.0 + csc2) + csh2, w1[l], b1[l], w2[l], b2[l]),
                             ln2_g[l], ln2_b[l])
    return x
```

```python
import time
import numpy as np
import concourse.bass as bass
import concourse.mybir as mybir
from concourse.bass_utils import run_bass_kernel_spmd
from contextlib import ExitStack

F32 = mybir.dt.float32
BF16 = mybir.dt.bfloat16
ALU = mybir.AluOpType
AF = mybir.ActivationFunctionType
AX = mybir.AxisListType

COMPUTE = ("pe", "act", "dve", "pool")
ALLENG = ("pe", "act", "dve", "pool", "sp")

D = 1024
NCTX = 256
NLAT = 2048
TALL = NCTX + NLAT
NT = TALL // 128
DEPTH = 2
NB = 2
ALPHA = float((2 * DEPTH) ** 0.25)
EPS = 1e-5
N_IN = 7216
O_A, O_Q, O_K, O_V, O_O, O_GIF, O_CQ, O_CKV, O_KR, O_G = 0, 1024, 1536, 2048, 2560, 3072, 3088, 3856, 4112, 4144
MLA_SCALE = float(96 ** -0.5)
KSCALE = float(128 ** -0.5)


class DSem:
    def __init__(self, handle, name):
        self.h = handle
        self.count = 0
        self.name = name


class Prog:
    def __init__(self, nc, same_eng_sync=True):
        self.nc = nc
        self.es = ExitStack()
        self.ops = {e: [] for e in ALLENG}
        self.cnt = {e: 0 for e in COMPUTE}
        self.sem = {}
        for e in COMPUTE:
            self.sem[e] = self.es.enter_context(nc.semaphore("s_" + e))
        self.seen = {e: {} for e in ALLENG}
        self.last_w = {}
        self.readers = {}
        self.same_eng_sync = same_eng_sync
        self.dsems = []
        self.uid = 0
        self.phase_es = None
        self.layer_es = None

    def sbuf(self, name, shape, dtype):
        return self.es.enter_context(self.nc.sbuf_tensor(name, list(shape), dtype))

    def psum(self, name, shape, dtype=F32):
        return self.es.enter_context(self.nc.psum_tensor(name, list(shape), dtype))

    def dsem(self, name=None):
        name = name or f"d{len(self.dsems)}"
        d = DSem(self.es.enter_context(self.nc.semaphore(name)), name)
        self.dsems.append(d)
        return d

    def phase_begin(self):
        assert self.phase_es is None
        self.phase_es = ExitStack()
        self.uid += 1

    def psbuf(self, name, shape, dtype):
        return self.phase_es.enter_context(self.nc.sbuf_tensor(f"{name}_p{self.uid}", list(shape), dtype))

    def phase_end(self):
        self.barrier()
        self.phase_es.close()
        self.phase_es = None

    def layer_begin(self):
        assert self.layer_es is None and self.phase_es is None
        self.layer_es = ExitStack()

    def lsbuf(self, name, shape, dtype):
        assert self.phase_es is None
        self.uid += 1
        return self.layer_es.enter_context(self.nc.sbuf_tensor(f"{name}_l{self.uid}", list(shape), dtype))

    def layer_end(self):
        assert self.phase_es is None
        self.layer_es.close()
        self.layer_es = None

    def seal(self, dsem, keys):
        for k in keys:
            self.last_w[k] = (dsem, dsem.count)

    def _deps(self, eng, reads, writes):
        deps = {}

        def add(ev):
            if ev is None:
                return
            k, v = ev
            if deps.get(k, 0) < v:
                deps[k] = v

        for r in reads:
            add(self.last_w.get(r))
        for w in writes:
            add(self.last_w.get(w))
            for ev in self.readers.get(w, ()):
                add(ev)
        out = []
        for k, v in deps.items():
            if isinstance(k, str) and k == eng and (eng == "pe" or not self.same_eng_sync):
                continue
            if self.seen[eng].get(k, 0) >= v:
                continue
            self.seen[eng][k] = v
            out.append((k, v))
        return out

    def _record(self, ev, reads, writes):
        for r in reads:
            lst = self.readers.setdefault(r, [])
            lst[:] = [x for x in lst if x[0] != ev[0]]
            lst.append(ev)
        for w in writes:
            self.last_w[w] = ev
            self.readers[w] = []

    def _h(self, k):
        return self.sem[k] if isinstance(k, str) else k.h

    def op(self, eng, fn, r=(), w=(), inc=True, force_self=False):
        w = list(w) + [k for k in r if k.startswith("ps") and k[2:].isdigit() and k not in w]
        waits = self._deps(eng, r, w)
        if force_self and self.cnt[eng] > 0 and self.seen[eng].get(eng, 0) < self.cnt[eng]:
            self.seen[eng][eng] = self.cnt[eng]
            waits.append((eng, self.cnt[eng]))
        if inc:
            self.cnt[eng] += 1
            ev = (eng, self.cnt[eng])
        else:
            ev = (eng, self.cnt[eng] + 1)
        self._record(ev, r, w)
        self.ops[eng].append((waits, fn, inc, None))
        return ev

    def dma(self, q, fn, dsem, r=(), w=()):
        waits = self._deps(q, r, w)
        dsem.count += 16
        ev = (dsem, dsem.count)
        self._record(ev, r, w)
        self.ops[q].append((waits, fn, False, dsem))
        return ev

    def wait_all(self, eng, evs):
        waits = []
        for k, v in evs:
            if isinstance(k, str) and k == eng:
                continue
            if self.seen[eng].get(k, 0) >= v:
                continue
            self.seen[eng][k] = v
            waits.append((k, v))
        if waits:
            self.ops[eng].append((waits, None, False, None))

    def barrier(self):
        evs = [(e, self.cnt[e]) for e in COMPUTE if self.cnt[e] > 0]
        evs += [(d, d.count) for d in self.dsems if d.count > 0 and not getattr(d, "nobarrier", False)]
        for e in ALLENG:
            self.wait_all(e, evs)

    def emit(self):
        nc = self.nc
        engmap = {"pe": "tensor", "act": "scalar", "dve": "vector", "pool": "gpsimd", "sp": "sync"}
        with nc.Block() as block:
            for e in ALLENG:
                ops = self.ops[e]
                if not ops:
                    continue

                def body(engine, ops=ops, e=e):
                    for waits, fn, inc, dsem in ops:
                        for k, v in waits:
                            engine.wait_ge(self._h(k), v)
                        if fn is None:
                            continue
                        ins = fn(engine)
                        if dsem is not None:
                            ins.then_inc(dsem.h, 16)
                        elif inc:
                            ins.then_inc(self.sem[e], 1)

                getattr(block, engmap[e])(body)

    def close(self):
        self.es.close()


def tkeys(pfx, n0, ln):
    return [f"{pfx}{t}" for t in range(n0 // 128, (n0 + ln + 127) // 128)]


class Builder:
    def __init__(self, nlayers=DEPTH, nb=NB, debug=None, stop_after=None):
        self.nl = nlayers
        self.nb = nb
        self.debug = debug or set()
        self.stop_after = stop_after
        self.nc = bass.Bass("TRN2", target_bir_lowering=False)
        self.P = Prog(self.nc)
        self.dbg_outs = {}

    def MM(self, ps_ap, ps_key, pairs, rkeys, start=True, stop=True, serialize=False):
        n = len(pairs)
        for i, (l, r) in enumerate(pairs):
            last = i == n - 1
            self.P.op("pe", lambda e, l=l, r=r, i=i, last=last: e.matmul(ps_ap, lhsT=l, rhs=r, start=(start and i == 0), stop=(stop and last)),
                      r=rkeys, w=[ps_key], inc=True, force_self=serialize)

    def TR(self, ps_ap, ps_key, in_ap, ident_ap, rkeys):
        self.P.op("pe", lambda e: e.transpose(ps_ap, in_ap, ident_ap), r=rkeys, w=[ps_key])

    def ACT(self, out, in_, func, r, w, bias=None, scale=None):
        kw = {}
        if bias is not None:
            kw["bias"] = bias
        if scale is not None:
            kw["scale"] = scale
        self.P.op("act", lambda e: e.activation(out=out, in_=in_, func=func, **kw), r=r, w=w)

    def TS(self, eng, out, in0, s1, s2, op0, op1, r, w):
        if op1 is None and eng == "pool" and op0 in (ALU.mult, ALU.add):
            op1 = ALU.add if op0 == ALU.mult else ALU.mult
            s2 = 0.0 if op0 == ALU.mult else 1.0
        if op1 is None:
            self.P.op(eng, lambda e: e.tensor_scalar(out=out, in0=in0, scalar1=s1, scalar2=None, op0=op0), r=r, w=w)
        else:
            self.P.op(eng, lambda e: e.tensor_scalar(out=out, in0=in0, scalar1=s1, scalar2=s2, op0=op0, op1=op1), r=r, w=w)

    def TT(self, eng, out, in0, in1, op, r, w):
        self.P.op(eng, lambda e: e.tensor_tensor(out=out, in0=in0, in1=in1, op=op), r=r, w=w)

    def STT(self, out, in0, scalar, in1, op0, op1, r, w):
        self.P.op("dve", lambda e: e.scalar_tensor_tensor(out=out, in0=in0, scalar=scalar, in1=in1, op0=op0, op1=op1), r=r, w=w)

    def CP(self, eng, out, in_, r, w):
        if eng == "act":
            self.P.op("act", lambda e: e.copy(out=out, in_=in_), r=r, w=w)
        else:
            self.P.op(eng, lambda e: e.tensor_copy(out=out, in_=in_), r=r, w=w)

    def MEMSET(self, eng, ap, val, w):
        self.P.op(eng, lambda e: e.memset(ap, val), w=w)

    def DMA(self, q, out, in_, dsem, r=(), w=(), **kw):
        return self.P.dma(q, lambda e: e.dma_start(out=out, in_=in_, **kw), dsem, r=r, w=w)

    def declare(self):
        nc = self.nc
        L = DEPTH

        def inp(name, shape):
            return nc.dram_tensor(name, list(shape), F32, kind="ExternalInput").ap()

        self.x = inp("x", [NB, NLAT, D])
        self.c = inp("c", [NB, D])
        self.ctx = inp("ctx", [NB, NCTX, D])
        self.c_ctx = inp("c_ctx", [D])
        self.w_mod = inp("w_mod", [L, D, 6 * D])
        self.b_mod = inp("b_mod", [L, 6 * D])
        self.w_in = inp("w_in", [L, D, N_IN])
        self.b_in = inp("b_in", [L, N_IN])
        self.w_dw = inp("w_dw", [L, 31, 512])
        self.b_dw = inp("b_dw", [L, 512])
        self.cn_g = inp("conv_norm_g", [L, 512])
        self.cn_b = inp("conv_norm_b", [L, 512])
        self.w_conv_out = inp("w_conv_out", [L, 512, D])
        self.ml_g = inp("mlstm_norm_g", [L, 512])
        self.w_ml_out = inp("w_mlstm_out", [L, 512, D])
        self.qn_g = inp("q_norm_g", [L, 768])
        self.w_uq = inp("w_uq", [L, 768, 768])
        self.kvn_g = inp("kv_norm_g", [L, 256])
        self.w_ukv = inp("w_ukv", [L, 256, 1024])
        self.w_mla_out = inp("w_mla_out", [L, 512, D])
        self.w_out = inp("w_out", [L, D, D])
        self.b_out = inp("b_out", [L, D])
        self.ln1_g = inp("ln1_g", [L, D])
        self.ln1_b = inp("ln1_b", [L, D])
        self.w1 = inp("w1", [L, D, 4 * D])
        self.b1 = inp("b1", [L, 4 * D])
        self.w2 = inp("w2", [L, 4 * D, D])
        self.b2 = inp("b2", [L, D])
        self.ln2_g = inp("ln2_g", [L, D])
        self.ln2_b = inp("ln2_b", [L, D])
        self.rope_c = inp("rope_c", [32, NLAT])
        self.rope_s = inp("rope_s", [32, NLAT])
        self.out = nc.dram_tensor("out", [NB, NLAT, D], F32, kind="ExternalOutput").ap()

        def scr(name, shape):
            return nc.dram_tensor(name, list(shape), BF16, kind="Internal").ap()

        self.S = {}
        for l in range(self.nl):
            self.S[l] = dict(
                mod=scr(f"s_mod{l}", [12, 128, 8 * 512]),
                in_kv=scr(f"s_inkv{l}", [1, 128, 8 * 320]),
                ukv=scr(f"s_ukv{l}", [2, 128, 2 * 512]),
                in_cq=scr(f"s_incq{l}", [2, 128, 8 * 384]),
                uq=scr(f"s_uq{l}", [2, 128, 6 * 512]),
                in_a=scr(f"s_ina{l}", [2, 128, 8 * 512]),
                in_gif=scr(f"s_ingif{l}", [1, 128, 8 * 16]),
                in_head=scr(f"s_inhead{l}", [4, 128, 8 * 512]),
                merge=scr(f"s_merge{l}", [6, 128, 12 * 512]),
                wout=scr(f"s_wout{l}", [2, 128, 8 * 512]),
                w1=scr(f"s_w1{l}", [8, 128, 8 * 512]),
                w2=scr(f"s_w2{l}", [4, 128, 32 * 256]),
            )

    def dbg_out(self, name, shape, dtype=F32):
        t = self.nc.dram_tensor("dbg_" + name, list(shape), dtype, kind="ExternalOutput").ap()
        self.dbg_outs[name] = t
        return t

    def precast(self):
        P = self.P
        self.Ssem = {}
        self.pc_hist = []
        self.pending_casts = []
        for l in range(self.nl):
            for name in ("mod", "in_kv", "ukv", "in_cq", "uq", "in_a", "in_gif", "in_head", "merge", "wout", "w1", "w2"):
                self.pending_casts.append((l, name))

    def emit_casts(self, n):
        for _ in range(n):
            if not self.pending_casts:
                return
            l, name = self.pending_casts.pop(0)
            self._emit_cast_group(l, name)

    def ensure_cast(self, l, name):
        while (l, name) in self.pending_casts:
            self.emit_casts(1)

    def _emit_cast_group(self, l, name):
        P = self.P
        S = self.S[l]
        hist = self.pc_hist
        if len(hist) >= 2:
            pd = hist[-2]
            P.wait_all("pool", [(pd, pd.count)])
        d = P.dsem(f"pc_{name}{l}")
        d.nobarrier = True
        self.Ssem[(l, name)] = d
        hist.append(d)
        key = f"S{l}{name}"

        def cast(dst, src):
            self.DMA("pool", dst, src, d, w=[key])

        def v3(ap2d, kt, n):
            return ap2d.rearrange("p (k n) -> p k n", n=n)

        def rows(w2d, c0, n):
            return w2d.rearrange("(k p) n -> p k n", p=128)[:, :, c0:c0 + n]

        if name == "mod":
            for s_ in range(12):
                cast(v3(S["mod"][s_], 8, 512), rows(self.w_mod[l], s_ * 512, 512))
        elif name == "in_kv":
            dst = v3(S["in_kv"][0], 8, 320)
            cast(dst[:, :, 0:256], rows(self.w_in[l], O_CKV, 256))
            cast(dst[:, :, 256:288], rows(self.w_in[l], O_KR, 32))
        elif name == "ukv":
            src4 = self.w_ukv[l].rearrange("(k p) (h e) -> p k h e", p=128, e=128)
            for k in range(2):
                cast(v3(S["ukv"][0], 2, 512)[:, k, :].rearrange("p (h e) -> p h e", e=64), src4[:, k, :, 0:64])
                cast(v3(S["ukv"][1], 2, 512)[:, k, :].rearrange("p (h e) -> p h e", e=64), src4[:, k, :, 64:128])
        elif name == "in_cq":
            for s_ in range(2):
                cast(v3(S["in_cq"][s_], 8, 384), rows(self.w_in[l], O_CQ + s_ * 384, 384))
        elif name == "uq":
            src4 = self.w_uq[l].rearrange("(k p) (h e) -> p k h e", p=128, e=96)
            for k in range(6):
                cast(v3(S["uq"][0], 6, 512)[:, k, :].rearrange("p (h e) -> p h e", e=64), src4[:, k, :, 0:64])
                d1 = v3(S["uq"][1], 6, 512)[:, k, :]
                cast(d1[:, 0:256].rearrange("p (h e) -> p h e", e=32), src4[:, k, :, 64:96])
        elif name == "in_a":
            for s_ in range(2):
                dst = v3(S["in_a"][s_], 8, 512)
                cast(dst[:, :, 0:256], rows(self.w_in[l], O_A + s_ * 256, 256))
                cast(dst[:, :, 256:512], rows(self.w_in[l], O_A + 512 + s_ * 256, 256))
        elif name == "in_gif":
            cast(v3(S["in_gif"][0], 8, 16), rows(self.w_in[l], O_GIF, 16))
        elif name == "in_head":
            for h in range(4):
                dst = v3(S["in_head"][h], 8, 512)
                for i, o in enumerate((O_Q, O_K, O_V, O_O)):
                    cast(dst[:, :, i * 128:(i + 1) * 128], rows(self.w_in[l], o + h * 128, 128))
        elif name == "merge":
            for br, wb in enumerate((self.w_conv_out, self.w_ml_out, self.w_mla_out)):
                for g in range(2):
                    dst = v3(S["merge"][br * 2 + g], 12, 512)
                    cast(dst[:, 0:4, :], rows(wb[l], g * 512, 512))
                    cast(dst[:, 4:12, :], rows(self.w_in[l], O_G + br * 1024 + g * 512, 512))
        elif name == "wout":
            for s_ in range(2):
                cast(v3(S["wout"][s_], 8, 512), rows(self.w_out[l], s_ * 512, 512))
        elif name == "w1":
            for s_ in range(8):
                cast(v3(S["w1"][s_], 8, 512), rows(self.w1[l], s_ * 512, 512))
        elif name == "w2":
            for s_ in range(4):
                cast(v3(S["w2"][s_], 32, 256), rows(self.w2[l], s_ * 256, 256))

    def consts(self):
        P = self.P
        self.identf = P.sbuf("identf", [128, 128], F32)
        self.identb = P.sbuf("identb", [128, 128], BF16)
        self.onesf = P.sbuf("onesf", [128, 128], F32)
        self.onesb = P.sbuf("onesb", [128, 128], BF16)
        self.maskf = P.sbuf("maskf", [128, 128], F32)
        self.maskb = P.sbuf("maskb", [128, 128], F32)
        self.MEMSET("pool", self.identf[:], 0.0, ["identf"])
        P.op("pool", lambda e: e.affine_select(out=self.identf[:], in_=self.identf[:], pattern=[[-1, 128]], compare_op=ALU.not_equal,
                                                fill=1.0, base=0, channel_multiplier=1), r=["identf"], w=["identf"])
        self.CP("pool", self.identb[:], self.identf[:], ["identf"], ["identb"])
        self.MEMSET("pool", self.onesf[:], 1.0, ["onesf"])
        self.MEMSET("pool", self.onesb[:], 1.0, ["onesb"])
        self.MEMSET("pool", self.maskf[:], 1.0, ["maskf"])
        P.op("pool", lambda e: e.affine_select(out=self.maskf[:], in_=self.maskf[:], pattern=[[1, 128]], compare_op=ALU.is_ge,
                                                fill=0.0, base=0, channel_multiplier=-1), r=["maskf"], w=["maskf"])
        self.MEMSET("pool", self.maskb[:], 1.0, ["maskb"])
        P.op("pool", lambda e: e.affine_select(out=self.maskb[:], in_=self.maskb[:], pattern=[[-1, 128]], compare_op=ALU.is_ge,
                                                fill=0.0, base=0, channel_multiplier=1), r=["maskb"], w=["maskb"])
        self.ps = [P.psum(f"ps{i}", [128, 512]) for i in range(8)]
        self.setup_sem = P.dsem("setup")
        self.misc_sem = P.dsem("misc")
        self.dbg_sem = P.dsem("dbg")
        self.stg_sems = [P.dsem(f"stg{i}") for i in range(2)]
        self.ost_sems = [P.dsem(f"ost{i}") for i in range(2)]

    def load_vecs(self):
        P = self.P
        self.VEC = {}
        self.VC = {}
        self.gifb = {}
        allsegs = {}
        for l in range(self.nl):
            segs = []
            bi = self.b_in[l]
            segs.append(("b_a", bi[O_A:O_A + 1024], 1024))
            segs.append(("b_qkv", bi[O_Q:O_Q + 1536], 1536))
            segs.append(("b_o", bi[O_O:O_O + 512], 512))
            segs.append(("b_cq", bi[O_CQ:O_CQ + 768], 768))
            segs.append(("b_ckv", bi[O_CKV:O_CKV + 256], 256))
            segs.append(("b_kr", bi[O_KR:O_KR + 32], 32))
            segs.append(("b_krsw", None, 32))
            segs.append(("b_g", bi[O_G:O_G + 3072], 3072))
            segs.append(("b_mod", self.b_mod[l], 6144))
            segs.append(("b1", self.b1[l], 4096))
            for nm, t in (("b_out", self.b_out), ("b2", self.b2), ("ln1_g", self.ln1_g), ("ln1_b", self.ln1_b),
                          ("ln2_g", self.ln2_g), ("ln2_b", self.ln2_b)):
                segs.append((nm, t[l], 1024))
            segs.append(("qn_g", self.qn_g[l], 768))
            segs.append(("kvn_g", self.kvn_g[l], 256))
            segs.append(("cn_g", self.cn_g[l], 512))
            segs.append(("cn_b", self.cn_b[l], 512))
            segs.append(("b_dw", self.b_dw[l], 512))
            segs.append(("ml_g", self.ml_g[l], 512))
            segs.append(("w_dw", None, 31 * 512))
            allsegs[l] = segs
            nrows = sum((n + 127) // 128 for _, _, n in segs)
            ngrp = (nrows + 127) // 128
            self.VEC[l] = P.sbuf(f"vec{l}", [128, ngrp * 128], F32)
            self.gifb[l] = P.sbuf(f"gifb{l}", [128, 16], F32)
        P.phase_begin()
        d = self.setup_sem
        todo = []
        for l in range(self.nl):
            segs = allsegs[l]
            bi = self.b_in[l]
            nrows = sum((n + 127) // 128 for _, _, n in segs)
            ngrp = (nrows + 127) // 128
            raw = [P.psbuf(f"raw{l}_{g}", [128, 128], F32) for g in range(ngrp)]
            rkeys = [f"raw{l}_{g}" for g in range(ngrp)]
            for g in range(ngrp):
                self.MEMSET("pool", raw[g][:], 0.0, [rkeys[g]])
            cols = {}
            r = 0
            for name, ap1d, n in segs:
                nr = (n + 127) // 128
                cols[name] = r
                rr = 0
                while rr < nr:
                    g, off = divmod(r + rr, 128)
                    take = min(nr - rr, 128 - off)
                    dst = raw[g][off:off + take, :]
                    key = rkeys[g]
                    if name == "w_dw":
                        src = self.w_dw[l].rearrange("k (b p) -> (k b) p", p=128)[rr:rr + take, :]
                        self.DMA("sp", dst, src, d, w=[key])
                    elif name == "b_krsw":
                        for rep in (0, 64):
                            for a in range(2):
                                for j in range(2):
                                    o = rep + a * 16 + j * 8
                                    so = O_KR + a * 16 + (1 - j) * 8
                                    self.DMA("sp", raw[g][off:off + 1, o:o + 8], bi[so:so + 8].unsqueeze(0), d, w=[key])
                    elif name == "b_kr":
                        for rep in (0, 64):
                            self.DMA("sp", raw[g][off:off + 1, rep:rep + n], ap1d.unsqueeze(0), d, w=[key])
                    elif n < 128:
                        self.DMA("sp", raw[g][off:off + 1, 0:n], ap1d.unsqueeze(0), d, w=[key])
                    else:
                        src = ap1d.rearrange("(c p) -> c p", p=128)[rr:rr + take, :]
                        self.DMA("sp", dst, src, d, w=[key])
                    rr += take
                r += nr
            self.VC[l] = cols
            self.DMA("sp", self.gifb[l][:], bi[O_GIF:O_GIF + 16].partition_broadcast(128), d, w=[f"gifb{l}"])
            todo.append((l, raw, rkeys, nrows, ngrp))
        allkeys = [k for (_, _, rk, _, _) in todo for k in rk] + [f"gifb{l}" for l in range(self.nl)]
        P.seal(d, allkeys)
        for (l, raw, rkeys, nrows, ngrp) in todo:
            for g in range(ngrp):
                n_g = min(128, nrows - g * 128)
                ps = self.ps[g % 8]
                self.TR(ps[:, 0:n_g], f"ps{g % 8}", raw[g][0:n_g, :], self.identf[0:n_g, 0:n_g], [rkeys[g], "identf"])
                self.CP("dve", self.VEC[l][:, g * 128:g * 128 + n_g], ps[:, 0:n_g], [f"ps{g % 8}"], [f"vec{l}"])
        P.phase_end()

    def V(self, l, name, j=0, n=1):
        c = self.VC[l][name] + j
        return self.VEC[l][:, c:c + n]

    def make_ring(self, name, nslots, elems):
        P = self.P
        ring = dict(name=name, n=nslots, i=0,
                    tiles=[P.psbuf(f"{name}{i}", [128, elems], BF16) for i in range(nslots)],
                    keys=[f"{name}{i}_p{P.uid}" for i in range(nslots)])
        if not hasattr(self, "ring_sems"):
            self.ring_sems = [P.dsem(f"ring{i}") for i in range(4)]
        return ring

    def ring_load(self, ring, l, grp, slab, nelem):
        self.ensure_cast(l, grp)
        s = ring["i"] % ring["n"]
        ring["i"] += 1
        t = ring["tiles"][s]
        key = ring["keys"][s]
        self.DMA("sp", t[:, 0:nelem], self.S[l][grp][slab], self.ring_sems[s], r=[f"S{l}{grp}"], w=[key])
        return t, key

    def fm_layernorm(self, zs, zkeys, outs, outkeys, ln, g_cols, b_cols, nfeat, tmp, func=AF.Identity, psA=6, psB=7):
        P = self.P
        nch = len(zs)
        sq, mean, rstd, t1 = tmp["sq"], tmp["mean"], tmp["rstd"], tmp["t1"]
        psa, psb = self.ps[psA], self.ps[psB]
        for i in range(nch):
            self.MM(psa[:, 0:ln], f"ps{psA}", [(self.onesf[:], zs[i])], ["onesf"] + zkeys[i], start=(i == 0), stop=(i == nch - 1))
        for i in range(nch):
            self.ACT(sq[:, 0:ln], zs[i], AF.Square, zkeys[i], ["ln_sq"])
            self.MM(psb[:, 0:ln], f"ps{psB}", [(self.onesf[:], sq[:, 0:ln])], ["onesf", "ln_sq"], start=(i == 0), stop=(i == nch - 1))
        inv = 1.0 / nfeat
        self.ACT(mean[:, 0:ln], psa[:, 0:ln], AF.Copy, [f"ps{psA}"], ["ln_mean"], scale=inv)
        self.TT("dve", t1[:, 0:ln], mean[:, 0:ln], mean[:, 0:ln], ALU.mult, ["ln_mean"], ["ln_t1"])
        self.STT(rstd[:, 0:ln], psb[:, 0:ln], inv, t1[:, 0:ln], ALU.mult, ALU.subtract, [f"ps{psB}", "ln_t1"], ["ln_rstd"])
        self.TS("dve", rstd[:, 0:ln], rstd[:, 0:ln], EPS, None, ALU.add, None, ["ln_rstd"], ["ln_rstd"])
        self.ACT(rstd[:, 0:ln], rstd[:, 0:ln], AF.Sqrt, ["ln_rstd"], ["ln_rstd"])
        P.op("dve", lambda e: e.reciprocal(out=rstd[:, 0:ln], in_=rstd[:, 0:ln]), r=["ln_rstd"], w=["ln_rstd"])
        for i in range(nch):
            self.TT("dve", t1[:, 0:ln], zs[i], mean[:, 0:ln], ALU.subtract, zkeys[i] + ["ln_mean"], ["ln_t1"])
            self.TT("pool", t1[:, 0:ln], t1[:, 0:ln], rstd[:, 0:ln], ALU.mult, ["ln_t1", "ln_rstd"], ["ln_t1"])
            self.ACT(outs[i], t1[:, 0:ln], func, ["ln_t1"], outkeys[i], bias=b_cols[i], scale=g_cols[i])

    def make_uT(self, uT, ukey, n0, ln, who, vidx):
        self.emit_casts(1)
        engs = ("act", "dve", "pool", "act", "dve", "act", "dve", "pool")
        for fc in range(8):
            wc = self.cur_b if who == 0 else 2
            sh = self.modT[:, vidx * 8 + fc, wc:wc + 1]
            scp = self.modP[:, (vidx // 3) * 8 + fc, wc:wc + 1]
            rk = tkeys("xT", n0, ln) + list(self.mk)
            if engs[fc] == "act":
                self.ACT(uT[:, fc, 0:ln], self.xT[:, fc, n0:n0 + ln], AF.Identity, rk, [ukey], bias=sh, scale=scp)
            else:
                self.TS(engs[fc], uT[:, fc, 0:ln], self.xT[:, fc, n0:n0 + ln], scp, sh, ALU.mult, ALU.add, rk, [ukey])

    def build(self):
        P = self.P
        self.declare()
        self.consts()
        self.precast()
        self.pending_casts.remove((0, "mod"))
        self.pending_casts.insert(self.pending_casts.index((0, "w2")) + 1, (0, "mod"))
        self.emit_casts(2)
        self.load_vecs()
        self.emit_casts(2)
        self.xT = P.sbuf("xT", [128, 8, TALL], F32)
        self.modTs = [P.sbuf(f"modT{l}", [128, 48, 3], F32) for l in range(self.nl)]
        self.modPs = [P.sbuf(f"modP{l}", [128, 32, 3], F32) for l in range(self.nl)]
        self.brT = [None, None, None]
        self.out_evs = []
        for b in range(self.nb):
            self.load_x(b)
            for l in range(self.nl):
                last = l == DEPTH - 1
                self.cur_b, self.cur_l = b, l
                self.modT, self.modP = self.modTs[l], self.modPs[l]
                self.mk = (f"modT{l}", f"modP{l}")
                if b == 0:
                    self.mod_vectors(b, l)
                if self.stop_after == "mod":
                    break
                P.layer_begin()
                stop = False
                for i, (nm, fn) in enumerate((("oT", self.phase_C), ("hcT", self.phase_A), ("hmT", self.phase_B))):
                    self.brT[i] = P.lsbuf(nm, [128, 4, TALL], BF16)
                    fn(b, l, last)
                    if self.stop_after == "CAB"[i]:
                        stop = True
                        break
                if not stop:
                    self.phase_D1(b, l, last)
                P.layer_end()
                if stop or self.stop_after == "D1":
                    break
                self.phase_D2(b, l, last)
            self.store_x(b)
        P.barrier()
        P.wait_all("sp", self.out_evs[-1:])
        P.emit()
        P.close()
        return self.nc

    def load_x(self, b):
        P = self.P
        P.phase_begin()
        st = [P.psbuf(f"xst{i}", [128, D], F32) for i in range(2)]
        for t in range(NT):
            s = st[t % 2]
            key = f"xst{t % 2}_p{P.uid}"
            src = self.ctx[b, t * 128:(t + 1) * 128, :] if t < 2 else self.x[b, (t - 2) * 128:(t - 1) * 128, :]
            self.DMA("sp", s[:], src, self.stg_sems[t % 2], w=[key])
            for half in range(2):
                pi = (2 * t + half) % 8
                ps = self.ps[pi]
                for q in range(4):
                    fc = half * 4 + q
                    self.TR(ps[:, q * 128:(q + 1) * 128], f"ps{pi}", s[:, fc * 128:(fc + 1) * 128], self.identf[:], [key, "identf"])
                eng = "act" if half == 0 else "dve"
                self.CP(eng, self.xT[:, half * 4:half * 4 + 4, t * 128:(t + 1) * 128],
                        ps[:].rearrange("p (q n) -> p q n", n=128), [f"ps{pi}"], [f"xT{t}"])
        P.phase_end()

    def store_x(self, b):
        P = self.P
        P.phase_begin()
        st = [P.psbuf(f"ost{i}", [128, D], F32) for i in range(2)]
        for t in range(2, NT):
            s = st[t % 2]
            key = f"ost{t % 2}_p{P.uid}"
            for half in range(2):
                pi = (2 * t + half) % 8
                ps = self.ps[pi]
                for q in range(4):
                    fc = half * 4 + q
                    self.TR(ps[:, q * 128:(q + 1) * 128], f"ps{pi}", self.xT[:, fc, t * 128:(t + 1) * 128], self.identf[:], [f"xT{t}", "identf"])
                eng = "act" if half == 0 else "dve"
                self.CP(eng, s[:, half * 512:(half + 1) * 512], ps[:], [f"ps{pi}"], [key])
            ev = self.DMA("sp", self.out[b, (t - 2) * 128:(t - 1) * 128, :], s[:], self.ost_sems[t % 2], r=[key])
            self.out_evs.append(ev)
        P.phase_end()

    def mod_vectors(self, b, l):
        P = self.P
        P.phase_begin()
        craw = P.psbuf("craw", [24, 128], F32)
        csil = P.psbuf("csil", [128, 24], BF16)
        MT, MP = self.modTs[l], self.modPs[l]
        mtk, mpk = f"modT{l}", f"modP{l}"
        ck = f"craw_p{P.uid}"
        for bb in range(NB):
            self.DMA("sp", craw[bb * 8:bb * 8 + 8, :], self.c[bb].rearrange("(c p) -> c p", p=128), self.misc_sem, w=[ck])
        self.DMA("sp", craw[16:24, :], self.c_ctx.rearrange("(c p) -> c p", p=128), self.misc_sem, w=[ck])
        P.seal(self.misc_sem, [ck])
        self.TR(self.ps[0][:, 0:24], "ps0", craw[:, :], self.identf[0:24, 0:24], [ck, "identf"])
        self.ACT(csil[:], self.ps[0][:, 0:24], AF.Silu, ["ps0"], ["csil"])
        psm = self.ps[1]
        direct = (b == 0 and l == 0)
        if direct:
            fstg = [P.psbuf(f"fstg{i}", [128, 8 * 512], F32) for i in range(2)]
            bstg = [P.psbuf(f"bstg{i}", [128, 8 * 512], BF16) for i in range(2)]
            ceng = ("act", "dve", "pool", "act", "dve", "act", "dve", "pool")
        else:
            ring = self.make_ring("rmod", 2, 8 * 512)
            nxt = self.ring_load(ring, l, "mod", 0, 8 * 512)
        for s in range(12):
            if direct:
                i = s % 2
                fk, bk = f"fstg{i}_p{P.uid}", f"bstg{i}_p{P.uid}"
                src = self.w_mod[l].rearrange("(k p) n -> p k n", p=128)[:, :, s * 512:(s + 1) * 512]
                self.DMA("sp", fstg[i][:].rearrange("p (k n) -> p k n", n=512), src, self.stg_sems[i], w=[fk])
                for k in range(8):
                    self.CP(ceng[k], bstg[i][:, k * 512:(k + 1) * 512], fstg[i][:, k * 512:(k + 1) * 512], [fk], [bk])
                t, key = bstg[i], bk
            else:
                t, key = nxt
                if s + 1 < 12:
                    nxt = self.ring_load(ring, l, "mod", s + 1, 8 * 512)
            wv = t[:].rearrange("p (k n) -> p k n", n=512)
            for q in range(4):
                col = s * 4 + q
                pairs = [(wv[:, k, q * 128:(q + 1) * 128], csil[:].rearrange("p (w k) -> p k w", w=3)[:, k, :]) for k in range(8)]
                self.MM(psm[:, col * 3:col * 3 + 3], "ps1", pairs, [key, "csil"])
        bm = self.V(l, "b_mod", 0, 48)
        self.TT("dve", MT[:], psm[:, 0:144].rearrange("p (c w) -> p c w", w=3), bm.unsqueeze(2).to_broadcast([128, 48, 3]), ALU.add,
                ["ps1", f"vec{l}"], [mtk])
        self.TS("dve", MP[:, 0:8, :], MT[:, 8:16, :], 1.0, None, ALU.add, None, [mtk], [mpk])
        self.TS("dve", MP[:, 8:16, :], MT[:, 32:40, :], 1.0, None, ALU.add, None, [mtk], [mpk])
        self.TT("dve", MP[:, 16:24, :], MT[:, 16:24, :], self.V(l, "b_out", 0, 8).unsqueeze(2).to_broadcast([128, 8, 3]), ALU.mult,
                [mtk, f"vec{l}"], [mpk])
        self.TT("dve", MP[:, 24:32, :], MT[:, 40:48, :], self.V(l, "b2", 0, 8).unsqueeze(2).to_broadcast([128, 8, 3]), ALU.mult,
                [mtk, f"vec{l}"], [mpk])
        if "mod" in self.debug and b == 0 and l == 0:
            o = self.dbg_out("modT", [128, 144])
            self.DMA("sp", o, MT[:].rearrange("p c w -> p (c w)"), self.dbg_sem, r=[mtk])
        P.phase_end()

    def phase_C(self, b, l, last):
        P = self.P
        P.phase_begin()
        oT = self.brT[0]
        knT = P.psbuf("knT", [128, 4, TALL], BF16)
        krT = P.psbuf("krT", [128, TALL], BF16)
        Vaug = P.psbuf("Vaug", [128, NT, 8, 65], BF16)
        uT = P.psbuf("uT", [128, 8, 512], BF16)
        xgf = P.psbuf("xg", [128, 1536], BF16)
        sqf = P.psbuf("sqb", [128, 1536], BF16)
        rstd = P.psbuf("rstd", [128, 512], F32)
        rstm = P.psbuf("rstm", [128, 4], F32)
        t1 = P.psbuf("t1", [96, 512], F32)
        t2 = P.psbuf("t2", [96, 512], F32)
        ropeC = P.psbuf("ropeC", [96, NLAT], F32)
        ropeS = P.psbuf("ropeS", [96, NLAT], F32)
        ring = self.make_ring("rc", 2, 8 * 384)
        wswk = P.psbuf("wswk", [128, 8, 32], BF16)
        wswq = P.psbuf("wswq", [128, 6, 256], BF16)
        uid = P.uid
        K = lambda n: f"{n}_p{uid}"
        for rb in (0,):
            self.DMA("sp", ropeC[rb:rb + 32, :], self.rope_c, self.misc_sem, w=[K("ropeC")])
            self.DMA("sp", ropeS[rb:rb + 32, :], self.rope_s, self.misc_sem, w=[K("ropeS")])
        P.seal(self.misc_sem, [K("ropeC"), K("ropeS")])
        self.MEMSET("pool", krT[:], 0.0, [K("krT")])
        self.MEMSET("pool", Vaug[:, :, :, 64:65], 1.0, [K("Vaug")])
        g_kv = self.V(l, "kvn_g", 0, 2)
        b_ckv = self.V(l, "b_ckv", 0, 2)
        chunks = [(0, 256, 1)] + [(256 + 512 * i, 512, 0) for i in range(4)]
        xg = xgf[:, 0:1024].rearrange("p (k n) -> p k n", n=512)
        sqb = sqf[:, 0:1024].rearrange("p (k n) -> p k n", n=512)
        for (n0, ln, who) in chunks:
            self.make_uT(uT, K("uT"), n0, ln, who, 0)
            wt, wkey = self.ring_load(ring, l, "in_kv", 0, 8 * 320)
            w3 = wt[:, 0:8 * 320].rearrange("p (k n) -> p k n", n=320)
            for i in range(2):
                ps = self.ps[i]
                self.MM(ps[:, 0:ln], f"ps{i}", [(w3[:, k, i * 128:(i + 1) * 128], uT[:, k, 0:ln]) for k in range(8)], [wkey, K("uT")])
                self.TS("dve", xg[:, i, 0:ln], ps[:, 0:ln], b_ckv[:, i:i + 1], g_kv[:, i:i + 1], ALU.add, ALU.mult, [f"ps{i}", f"vec{l}"], [K("xg")])
                self.ACT(sqb[:, i, 0:ln], ps[:, 0:ln], AF.Square, [f"ps{i}", f"vec{l}"], [K("sqb")], bias=b_ckv[:, i:i + 1])
            ps_r, ps_s = self.ps[2], self.ps[3]
            if who == 0:
                srcv = w3[:, :, 256:288].rearrange("p k (a j f) -> p k a j f", a=2, j=2)
                dstv = wswk[:].rearrange("p k (a j f) -> p k a j f", a=2, j=2)
                for j in range(2):
                    self.CP("pool", dstv[:, :, :, j, :], srcv[:, :, :, 1 - j, :], [wkey], [K("wswk")])
            for rb in (0,):
                self.MM(ps_r[rb:rb + 32, 0:ln], "ps2", [(w3[:, k, 256:288], uT[:, k, 0:ln]) for k in range(8)], [wkey, K("uT")])
                if who == 0:
                    self.MM(ps_s[rb:rb + 32, 0:ln], "ps3", [(wswk[:, k, :], uT[:, k, 0:ln]) for k in range(8)], [K("wswk"), K("uT")])
            for rb in (0,):
                bkr = self.V(l, "b_kr")[rb:rb + 32, :]
                if who == 1:
                    self.TS("dve", krT[rb:rb + 32, n0:n0 + ln], ps_r[rb:rb + 32, 0:ln], bkr, None, ALU.add, None, ["ps2", f"vec{l}"], [K("krT")])
                else:
                    bsw = self.V(l, "b_krsw")[rb:rb + 32, :]
                    p0 = n0 - NCTX
                    self.STT(t1[rb:rb + 32, 0:ln], ps_r[rb:rb + 32, 0:ln], bkr, ropeC[rb:rb + 32, p0:p0 + ln], ALU.add, ALU.mult,
                             ["ps2", f"vec{l}", K("ropeC")], [K("t1")])
                    self.STT(t2[rb:rb + 32, 0:ln], ps_s[rb:rb + 32, 0:ln], bsw, ropeS[rb:rb + 32, p0:p0 + ln], ALU.add, ALU.mult,
                             ["ps3", f"vec{l}", K("ropeS")], [K("t2")])
                    self.TT("pool", krT[rb:rb + 32, n0:n0 + ln], t1[rb:rb + 32, 0:ln], t2[rb:rb + 32, 0:ln], ALU.add, [K("t1"), K("t2")], [K("krT")])
            self.MM(self.ps[4][:, 0:ln], "ps4", [(self.onesb[:], sqb[:, i, 0:ln]) for i in range(2)], ["onesb", K("sqb")])
            self.ACT(rstd[:, 0:ln], self.ps[4][:, 0:ln], AF.Sqrt, ["ps4"], [K("rstd")], bias=self.epsc[:, 0:1], scale=1.0 / 256)
            P.op("dve", lambda e, ln=ln: e.reciprocal(out=rstd[:, 0:ln], in_=rstd[:, 0:ln]), r=[K("rstd")], w=[K("rstd")])
            nt = ln // 128
            for tt in range(nt):
                self.MM(self.ps[5][:, tt:tt + 1], "ps5", [(sqb[:, i, tt * 128:(tt + 1) * 128], self.onesb[:, 0:1]) for i in range(2)], [K("sqb"), "onesb"])
            self.ACT(rstm[:, 0:nt], self.ps[5][:, 0:nt], AF.Sqrt, ["ps5"], [K("rstm")], bias=self.epsc[:, 0:1], scale=1.0 / 256)
            P.op("dve", lambda e, nt=nt: e.reciprocal(out=rstm[:, 0:nt], in_=rstm[:, 0:nt]), r=[K("rstm")], w=[K("rstm")])
            wt, wkey = self.ring_load(ring, l, "ukv", 0, 2 * 512)
            wn = wt[:, 0:1024].rearrange("p (k n) -> p k n", n=512)
            for p in range(4):
                pi = p % 2
                ps = self.ps[pi]
                self.MM(ps[:, 0:ln], f"ps{pi}", [(wn[:, k, p * 128:(p + 1) * 128], xg[:, k, 0:ln]) for k in range(2)], [wkey, K("xg")])
                self.TT("dve", knT[:, p, n0:n0 + ln], ps[:, 0:ln], rstd[:, 0:ln], ALU.mult, [f"ps{pi}", K("rstd")], [K("knT")])
            wt, wkey = self.ring_load(ring, l, "ukv", 1, 2 * 512)
            wv = wt[:, 0:1024].rearrange("p (k n) -> p k n", n=512)
            for tt in range(nt):
                pi = 6 + tt % 2
                ps = self.ps[pi]
                tile = (n0 // 128) + tt
                self.MM(ps[:, 0:512], f"ps{pi}", [(xg[:, k, tt * 128:(tt + 1) * 128], wv[:, k, :]) for k in range(2)], [wkey, K("xg")])
                self.ACT(Vaug[:, tile, :, 0:64], ps[:, 0:512].rearrange("p (h e) -> p h e", e=64), AF.Copy, [f"ps{pi}", K("rstm")], [K("Vaug")],
                         scale=rstm[:, tt:tt + 1])
        if "kv" in self.debug and b == 0 and l == 0:
            o = self.dbg_out("knT", [128, 4 * TALL], BF16)
            self.DMA("sp", o, knT[:].rearrange("p a n -> p (a n)"), self.dbg_sem, r=[K("knT")])
            o = self.dbg_out("krT", [32, TALL], BF16)
            self.DMA("sp", o, krT[0:32, :], self.dbg_sem, r=[K("krT")])
            o = self.dbg_out("Vaug", [128, NT * 8 * 65], BF16)
            self.DMA("sp", o, Vaug[:].rearrange("p a h e -> p (a h e)"), self.dbg_sem, r=[K("Vaug")])
        QC = 256
        xg = xgf[:].rearrange("p (k n) -> p k n", n=QC)
        sqb = sqf[:].rearrange("p (k n) -> p k n", n=QC)
        qnT = P.psbuf("qnT", [128, 8, QC], BF16)
        qrT = P.psbuf("qrT", [128, 8, QC], BF16)
        self.MEMSET("pool", qnT[:], 0.0, [K("qnT")])
        self.MEMSET("pool", qrT[:], 0.0, [K("qrT")])
        P.barrier()
        tq = [(t1[:, i * QC:(i + 1) * QC], t2[:, i * QC:(i + 1) * QC]) for i in range(2)]
        Cp = P.psbuf("Cp", [96, QC], F32)
        Sp = P.psbuf("Sp", [96, QC], F32)
        PT = [P.psbuf(f"PT{i}", [128, QC], BF16) for i in range(4)]
        otm = [P.psbuf(f"otm{i}", [128, 512], BF16) for i in range(2)]
        rden = P.psbuf("rden", [128, 2], F32)
        g_q = self.V(l, "qn_g", 0, 6)
        b_cq = self.V(l, "b_cq", 0, 6)
        qchunks = [(256 + QC * i, QC, 0) for i in range(NLAT // QC)]
        if not last:
            qchunks = [(0, 256, 1)] + qchunks
        pt_i = 0
        for (n0, ln, who) in qchunks:
            self.make_uT(uT, K("uT"), n0, ln, who, 0)
            for s in range(2):
                wt, wkey = self.ring_load(ring, l, "in_cq", s, 8 * 384)
                w3 = wt[:, 0:8 * 384].rearrange("p (k n) -> p k n", n=384)
                for q in range(3):
                    i = s * 3 + q
                    pi = i % 4
                    ps = self.ps[pi]
                    self.MM(ps[:, 0:ln], f"ps{pi}", [(w3[:, k, q * 128:(q + 1) * 128], uT[:, k, 0:ln]) for k in range(8)], [wkey, K("uT")])
                    self.TS("dve", xg[:, i, 0:ln], ps[:, 0:ln], b_cq[:, i:i + 1], g_q[:, i:i + 1], ALU.add, ALU.mult, [f"ps{pi}", f"vec{l}"], [K("xg")])
                    self.ACT(sqb[:, i, 0:ln], ps[:, 0:ln], AF.Square, [f"ps{pi}", f"vec{l}"], [K("sqb")], bias=b_cq[:, i:i + 1])
            self.MM(self.ps[4][:, 0:ln], "ps4", [(self.onesb[:], sqb[:, i, 0:ln]) for i in range(6)], ["onesb", K("sqb")])
            self.ACT(rstd[:, 0:ln], self.ps[4][:, 0:ln], AF.Sqrt, ["ps4"], [K("rstd")], bias=self.epsc[:, 0:1], scale=1.0 / 768)
            P.op("dve", lambda e, ln=ln: e.reciprocal(out=rstd[:, 0:ln], in_=rstd[:, 0:ln]), r=[K("rstd")], w=[K("rstd")])
            wt, wkey = self.ring_load(ring, l, "uq", 0, 6 * 512)
            wqn = wt[:, 0:3072].rearrange("p (k n) -> p k n", n=512)
            for p in range(4):
                pi = 5 + p % 2
                ps = self.ps[pi]
                self.MM(ps[:, 0:ln], f"ps{pi}", [(wqn[:, k, p * 128:(p + 1) * 128], xg[:, k, 0:ln]) for k in range(6)], [wkey, K("xg")])
                for hh in range(2):
                    self.TT("dve", qnT[hh * 64:hh * 64 + 64, 2 * p + hh, 0:ln], ps[hh * 64:hh * 64 + 64, 0:ln], rstd[hh * 64:hh * 64 + 64, 0:ln],
                            ALU.mult, [f"ps{pi}", K("rstd")], [K("qnT")])
            if who == 0:
                p0 = n0 - NCTX
                for rb in (0,):
                    self.TT("pool", Cp[rb:rb + 32, 0:ln], ropeC[rb:rb + 32, p0:p0 + ln], rstd[rb:rb + 32, 0:ln], ALU.mult, [K("ropeC"), K("rstd")], [K("Cp")])
                    self.TT("pool", Sp[rb:rb + 32, 0:ln], ropeS[rb:rb + 32, p0:p0 + ln], rstd[rb:rb + 32, 0:ln], ALU.mult, [K("ropeS"), K("rstd")], [K("Sp")])
            wt, wkey = self.ring_load(ring, l, "uq", 1, 6 * 512)
            wqr = wt[:, 0:3072].rearrange("p (k n) -> p k n", n=512)
            if who == 0:
                for k in range(6):
                    srcv = wqr[:, k, 0:256].rearrange("p (h a j f) -> p h a j f", h=8, a=2, j=2)
                    dstv = wswq[:, k, :].rearrange("p (h a j f) -> p h a j f", h=8, a=2, j=2)
                    for j in range(2):
                        self.CP("pool", dstv[:, :, :, j, :], srcv[:, :, :, 1 - j, :], [wkey], [K("wswq")])
            for h in range(8):
                rb = 0
                hp = h
                pr, psw = self.ps[(2 * h) % 4], self.ps[(2 * h + 1) % 4]
                kr_, ks_ = f"ps{(2 * h) % 4}", f"ps{(2 * h + 1) % 4}"
                self.MM(pr[rb:rb + 32, 0:ln], kr_, [(wqr[:, k, h * 32:(h + 1) * 32], xg[:, k, 0:ln]) for k in range(6)], [wkey, K("xg")])
                if who == 1:
                    self.TT("dve", qrT[rb:rb + 32, hp, 0:ln], pr[rb:rb + 32, 0:ln], rstd[rb:rb + 32, 0:ln], ALU.mult, [kr_, K("rstd")], [K("qrT")])
                else:
                    self.MM(psw[rb:rb + 32, 0:ln], ks_, [(wswq[:, k, h * 32:(h + 1) * 32], xg[:, k, 0:ln]) for k in range(6)], [K("wswq"), K("xg")])
                    ta, tb = tq[h % 2]
                    self.TT("dve", ta[0:32, 0:ln], pr[0:32, 0:ln], Cp[0:32, 0:ln], ALU.mult, [kr_, K("Cp")], [K(f"tqa{h % 2}")])
                    self.TT("dve", tb[0:32, 0:ln], psw[0:32, 0:ln], Sp[0:32, 0:ln], ALU.mult, [ks_, K("Sp")], [K(f"tqb{h % 2}")])
                    self.TT("pool", qrT[0:32, hp, 0:ln], ta[0:32, 0:ln], tb[0:32, 0:ln], ALU.add, [K(f"tqa{h % 2}"), K(f"tqb{h % 2}")], [K("qrT")])
            ktiles = list(range(2)) if who == 1 else list(range(NT))
            nk = len(ktiles)
            nq = ln // 128
            items = [(h, ki) for h in range(8) for ki in range(nk)]

            def issue_scores(j):
                h, ki = items[j]
                kt_ = ktiles[ki]
                hp, hb = h // 2, (h % 2) * 64
                si = j % 4
                pairs = [(knT[:, hp, kt_ * 128:(kt_ + 1) * 128], qnT[:, h, 0:ln]),
                         (krT[:, kt_ * 128:(kt_ + 1) * 128], qrT[:, h, 0:ln])]
                self.MM(self.ps[si][:, 0:ln], f"ps{si}", pairs, [K("knT"), K("qnT"), K("krT"), K("qrT")])

            LOOK = 3
            for j in range(min(LOOK, len(items))):
                issue_scores(j)
            for j, (h, ki) in enumerate(items):
                kt_ = ktiles[ki]
                si = j % 4
                ob = [self.ps[4 + (h % 2) * 2 + qt] for qt in range(nq)]
                obk = [f"ps{4 + (h % 2) * 2 + qt}" for qt in range(nq)]
                pt = PT[j % 4]
                ptk = K(f"PT{j % 4}")
                self.ACT(pt[:, 0:ln], self.ps[si][:, 0:ln], AF.Exp, [f"ps{si}"], [ptk], scale=MLA_SCALE)
                if j + LOOK < len(items):
                    issue_scores(j + LOOK)
                for qt in range(nq):
                    self.MM(ob[qt][:, 0:65], obk[qt], [(pt[:, qt * 128:(qt + 1) * 128], Vaug[:, kt_, h, :])], [ptk, K("Vaug")],
                            start=(ki == 0), stop=(ki == nk - 1))
                if ki == nk - 1:
                    for qt in range(nq):
                        P.op("dve", lambda e, qt=qt, o=ob[qt]: e.reciprocal(out=rden[:, qt:qt + 1], in_=o[:, 64:65]), r=[obk[qt]], w=[K("rden")])
                        self.TS("dve", otm[qt][:, h * 64:(h + 1) * 64], ob[qt][:, 0:64], rden[:, qt:qt + 1], None, ALU.mult, None,
                                [obk[qt], K("rden")], [K(f"otm{qt}")])
            for qt in range(nq):
                psb = self.ps[3][:].bitcast(BF16)
                for kk in range(4):
                    self.TR(psb[:, kk * 128:(kk + 1) * 128], "ps3", otm[qt][:, kk * 128:(kk + 1) * 128], self.identb[:], [K(f"otm{qt}"), "identb"])
                tok = n0 + qt * 128
                self.CP("act", oT[:, :, tok:tok + 128], psb[:, 0:512].rearrange("p (k n) -> p k n", n=128), ["ps3"], [f"oT{tok // 128}"])
        if "oT" in self.debug and b == 0 and l == 0:
            o = self.dbg_out("oT", [128, 4 * TALL], BF16)
            self.DMA("sp", o, oT[:].rearrange("p a n -> p (a n)"), self.dbg_sem, r=tkeys("oT", 0, TALL))
        P.phase_end()

    def phase_A(self, b, l, last):
        P = self.P
        P.phase_begin()
        hcT = self.brT[1]
        uid = P.uid
        K = lambda n: f"{n}_p{uid}"
        WC, WL = 15 + NCTX + 15, 15 + NLAT + 15
        Hb = P.psbuf("Hb", [128, 4, WC + WL], BF16)
        uT = P.psbuf("uT", [128, 8, 512], BF16)
        sg = P.psbuf("sg", [128, 512], F32)
        acc = P.psbuf("acc", [128, 4, 512], F32)
        dg = [P.psbuf(f"dg{i}", [128, 128], BF16) for i in range(6)]
        tmp = dict(sq=P.psbuf("lsq", [128, 512], F32), mean=P.psbuf("lmean", [128, 512], F32),
                   rstd=P.psbuf("lrstd", [128, 512], F32), t1=P.psbuf("lt1", [128, 512], F32))
        ring = self.make_ring("ra", 2, 8 * 512)
        for blk in range(4):
            for (o, n) in ((0, 15), (15 + NCTX, 15), (WC, 15), (WC + 15 + NLAT, 15)):
                self.MEMSET("pool", Hb[:, blk, o:o + n], 0.0, [K("Hb")])
        chunks = ([(0, 256, 1)] if not last else []) + [(256 + 512 * i, 512, 0) for i in range(4)]

        def hoff(n0):
            return 15 + n0 if n0 < NCTX else WC + 15 + (n0 - NCTX)

        b_a = self.V(l, "b_a", 0, 8)
        for (n0, ln, who) in chunks:
            self.make_uT(uT, K("uT"), n0, ln, who, 0)
            for s in range(2):
                wt, wkey = self.ring_load(ring, l, "in_a", s, 8 * 512)
                w3 = wt[:].rearrange("p (k n) -> p k n", n=512)
                for q in range(2):
                    blk = s * 2 + q
                    pv, pg = self.ps[(blk * 2) % 6], self.ps[(blk * 2 + 1) % 6]
                    kv, kg = f"ps{(blk * 2) % 6}", f"ps{(blk * 2 + 1) % 6}"
                    self.MM(pv[:, 0:ln], kv, [(w3[:, k, q * 128:(q + 1) * 128], uT[:, k, 0:ln]) for k in range(8)], [wkey, K("uT")])
                    self.MM(pg[:, 0:ln], kg, [(w3[:, k, 256 + q * 128:256 + (q + 1) * 128], uT[:, k, 0:ln]) for k in range(8)], [wkey, K("uT")])
                    self.ACT(sg[:, 0:ln], pg[:, 0:ln], AF.Sigmoid, [kg, f"vec{l}"], [K("sg")], bias=b_a[:, 4 + blk:5 + blk])
                    c0 = hoff(n0)
                    self.STT(Hb[:, blk, c0:c0 + ln], pv[:, 0:ln], b_a[:, blk:blk + 1], sg[:, 0:ln], ALU.add, ALU.mult,
                             [kv, K("sg"), f"vec{l}"], [K("Hb")])
        wdw = self.V(l, "w_dw", 0, 124)
        bdw = self.V(l, "b_dw", 0, 4)
        gcols = [self.V(l, "cn_g", i) for i in range(4)]
        bcols = [self.V(l, "cn_b", i) for i in range(4)]
        di = 0
        for (n0, ln, who) in chunks:
            c0 = hoff(n0) - 15
            for blk in range(4):
                ps = self.ps[blk]
                for k in range(31):
                    d_ = dg[di % 6]
                    dk = K(f"dg{di % 6}")
                    di += 1
                    wcol = wdw[:, k * 4 + blk:k * 4 + blk + 1]
                    if di % 2 == 0:
                        self.TS("dve", d_[:], self.identf[:], wcol, None, ALU.mult, None, ["identf", f"vec{l}"], [dk])
                    else:
                        self.ACT(d_[:], self.identf[:], AF.Copy, ["identf", f"vec{l}"], [dk], scale=wcol)
                    self.MM(ps[:, 0:ln], f"ps{blk}", [(d_[:], Hb[:, blk, c0 + k:c0 + k + ln])], [dk, K("Hb")], start=(k == 0), stop=(k == 30))
                self.ACT(acc[:, blk, 0:ln], ps[:, 0:ln], AF.Identity, [f"ps{blk}", f"vec{l}"], [K(f"acc{blk}")], bias=bdw[:, blk:blk + 1])
            self.fm_layernorm([acc[:, i, 0:ln] for i in range(4)], [[K(f"acc{i}")] for i in range(4)],
                              [hcT[:, i, n0:n0 + ln] for i in range(4)], [tkeys("hcT", n0, ln)] * 4, ln, gcols, bcols, 512.0, tmp, func=AF.Silu)
        if "hcT" in self.debug and b == 0 and l == 0:
            o = self.dbg_out("hcT", [128, 4 * TALL], BF16)
            self.DMA("sp", o, hcT[:].rearrange("p a n -> p (a n)"), self.dbg_sem, r=tkeys("hcT", 0, TALL))
        P.phase_end()

    def phase_B(self, b, l, last):
        P = self.P
        P.phase_begin()
        hmT = self.brT[2]
        uid = P.uid
        K = lambda n: f"{n}_p{uid}"
        uT = P.psbuf("uT", [128, 8, 512], BF16)
        ring = self.make_ring("rb", 2, 8 * 512)
        wg = P.psbuf("wg", [128, 8 * 16], BF16)
        self.ensure_cast(l, "in_gif")
        self.DMA("sp", wg[:], self.S[l]["in_gif"][0], self.misc_sem, r=[f"S{l}in_gif"], w=[K("wg")])
        P.seal(self.misc_sem, [K("wg")])
        wg3 = wg[:].rearrange("p (k n) -> p k n", n=16)
        pos_f = list(range(NT))
        order_b = [1, 0] + list(range(NT - 1, 1, -1))
        pos_b = [0] * NT
        for i, c in enumerate(order_b):
            pos_b[c] = i
        chunks = [(0, 256, 1)] + [(256 + 512 * i, 512, 0) for i in range(4)]
        qT = P.psbuf("qT", [128, TALL], BF16)
        kT = P.psbuf("kT", [128, TALL], BF16)
        vT = P.psbuf("vT", [128, 512], BF16)
        sgo = P.psbuf("sgo", [128, TALL], BF16)
        Va = P.psbuf("Va", [128, NT, 129], BF16)
        Hd = [P.psbuf(f"Hd{d}", [128, NT, 129], F32) for d in range(2)]
        rin = P.psbuf("rin", [128, 2, NT], F32)
        hflat = Hd[1][:].rearrange("p a b -> p (a b)")
        self.MEMSET("pool", Va[:, :, 128:129], 1.0, [K("Va")])
        GT = hflat[:, 1152:1152 + 288].rearrange("p (d k n) -> p d k n", d=2, k=2)
        G2 = [[hflat[0:72, 640 + (d * 2 + k) * 128:640 + (d * 2 + k + 1) * 128] for k in range(2)] for d in range(2)]
        gbias = P.psbuf("gbias", [72, 4], F32)
        lf, cum, dd, ee, th = [hflat[0:72, i * 128:(i + 1) * 128] for i in range(5)]
        col = P.psbuf("col", [72, 4], F32)
        rowM = P.psbuf("rowM", [1, 72], F32)
        rowB = P.psbuf("rowB", [1, 72], F32)
        rowS = P.psbuf("rowS", [1, 76], F32)
        rowMM = P.psbuf("rowMM", [1, 72], F32)
        rowA = P.psbuf("rowA", [1, 72], F32)
        mmc = P.psbuf("mmc", [72, 1], F32)
        ETAB = P.psbuf("ETAB", [128, 2, 72], F32)
        TTAB = P.psbuf("TTAB", [128, 2, 72], F32)
        ATAB = P.psbuf("ATAB", [128, 2, 72], F32)
        bi = self.b_in[l]
        for d in range(2):
            for k in range(2):
                o = O_GIF + (d * 2 + k) * 4
                for pp in range(NT):
                    pass
        Cst = [P.psbuf(f"Cst{d}", [128, 129], F32) for d in range(2)]
        Cop = [P.psbuf(f"Cop{d}", [128, 129], BF16) for d in range(2)]
        Ke2 = [[P.psbuf(f"Ke{d}{i}", [128, 128], BF16) for i in range(2)] for d in range(2)]
        Pm2 = [[P.psbuf(f"Pm{d}{i}", [128, 128], BF16) for i in range(2)] for d in range(2)]
        dn = P.psbuf("dn", [128, 8], F32)
        st6 = P.psbuf("st6", [128, 6], F32)
        mv = P.psbuf("mv", [128, 4], F32)
        st6a = P.psbuf("st6a", [128, NT, 6], F32)
        mva = P.psbuf("mva", [128, NT, 2], F32)
        rsa = P.psbuf("rsa", [128, NT], F32)
        b_qkv = self.V(l, "b_qkv", 0, 12)
        b_o = self.V(l, "b_o", 0, 4)
        mlg = self.V(l, "ml_g", 0, 4)
        gb = self.gifb[l]
        for h in range(4):
            for (n0, ln, who) in chunks:
                self.make_uT(uT, K("uT"), n0, ln, who, 0)
                wt, wkey = self.ring_load(ring, l, "in_head", h, 8 * 512)
                w3 = wt[:].rearrange("p (k n) -> p k n", n=512)
                nt = ln // 128
                outs = []
                for i in range(4):
                    ps = self.ps[i]
                    self.MM(ps[:, 0:ln], f"ps{i}", [(w3[:, k, i * 128:(i + 1) * 128], uT[:, k, 0:ln]) for k in range(8)], [wkey, K("uT")])
                bq = b_qkv[:, h:h + 1]
                bk = b_qkv[:, 4 + h:5 + h]
                bv = b_qkv[:, 8 + h:9 + h]
                self.TS("dve", qT[:, n0:n0 + ln], self.ps[0][:, 0:ln], bq, None, ALU.add, None, ["ps0", f"vec{l}"], [K("qT")])
                self.TS("dve", kT[:, n0:n0 + ln], self.ps[1][:, 0:ln], bk, KSCALE, ALU.add, ALU.mult, ["ps1", f"vec{l}"], [K("kT")])
                self.ACT(vT[:, 0:ln], self.ps[2][:, 0:ln], AF.Identity, ["ps2", f"vec{l}"], [K("vT")], bias=bv)
                self.ACT(sgo[:, n0:n0 + ln], self.ps[3][:, 0:ln], AF.Sigmoid, ["ps3", f"vec{l}"], [K("sgo")], bias=b_o[:, h:h + 1])
                psb = self.ps[4][:].bitcast(BF16)
                for tt in range(nt):
                    self.TR(psb[:, tt * 128:(tt + 1) * 128], "ps4", vT[:, tt * 128:(tt + 1) * 128], self.identb[:], [K("vT"), "identb"])
                t0 = n0 // 128
                self.CP("act", Va[:, t0:t0 + nt, 0:128], psb[:, 0:nt * 128].rearrange("p (t n) -> p t n", n=128), ["ps4"], [K("Va")])
                if h == 0:
                    for tt in range(nt):
                        c = t0 + tt
                        for d, pos in ((0, pos_f[c]), (1, pos_b[c])):
                            pg = self.ps[5]
                            self.MM(pg[:, d * 256 + pos * 8: d * 256 + pos * 8 + 8], "ps5",
                                    [(uT[:, k, tt * 128:(tt + 1) * 128], wg3[:, k, d * 8:d * 8 + 8]) for k in range(8)], [K("uT"), K("wg")])
            if h == 0:
                pg = self.ps[5]
                for d in range(2):
                    src = pg[:, d * 256:d * 256 + NT * 8].rearrange("p (c k h) -> p k c h", k=2, h=4)
                    bsrc = gb[:, d * 8:d * 8 + 8].rearrange("p (k h) -> p k h", h=4).unsqueeze(2).to_broadcast([128, 2, NT, 4])
                    self.TT("dve", GT[:, d, :, :].rearrange("p k (c h) -> p k c h", h=4), src, bsrc, ALU.add, ["ps5", f"gifb{l}"], [K("GT")])
                for d in range(2):
                    for k in range(2):
                        pi = 6 + k
                        self.TR(self.ps[pi][0:72, 0:128], f"ps{pi}", GT[:, d, k, :], self.identf[:], [K("GT"), "identf"])
                        self.CP("dve", G2[d][k][:], self.ps[pi][0:72, 0:128], [f"ps{pi}"], [K(f"G2_{d}{k}")])
                    gi, gf = G2[d][0], G2[d][1]
                    kgi, kgf = K(f"G2_{d}0"), K(f"G2_{d}1")
                    self.ACT(lf[:], gf[:], AF.Exp, [kgf], [K("lf")], scale=-1.0)
                    self.ACT(lf[:], lf[:], AF.Ln, [K("lf")], [K("lf")], bias=self.onec[0:72, 0:1])
                    self.TS("dve", lf[:], lf[:], -1.0, None, ALU.mult, None, [K("lf")], [K("lf")])
                    P.op("dve", lambda e: e.tensor_tensor_scan(out=cum[:], data0=self.onesf[0:72, :], data1=lf[:], initial=0.0, op0=ALU.mult, op1=ALU.add),
                         r=[K("lf"), "onesf"], w=[K("cum")])
                    self.CP("dve", col[:, 1:2], cum[:, 127:128], [K("cum")], [K("col")])
                    if d == 1:
                        self.TT("dve", cum[:], lf[:], cum[:], ALU.subtract, [K("lf"), K("cum")], [K("cum")])
                        self.TS("dve", cum[:], cum[:], col[:, 1:2], None, ALU.add, None, [K("cum"), K("col")], [K("cum")])
                    self.TT("dve", dd[:], gi[:], cum[:], ALU.subtract, [kgi, K("cum")], [K("dd")])
                    P.op("dve", lambda e: e.tensor_reduce(out=col[:, 0:1], in_=dd[:], axis=AX.X, op=ALU.max), r=[K("dd")], w=[K("col")])
                    self.TR(self.ps[6][0:2, 0:72], "ps6", col[:, 0:2], self.identf[0:72, 0:72], [K("col"), "identf"])
                    self.CP("dve", rowM[:], self.ps[6][0:1, 0:72], ["ps6"], [K("rowM")])
                    self.MM(self.ps[7][0:1, 0:72], "ps7", [(col[:, 1:2], self.identf[0:72, 0:72])], [K("col"), "identf"])
                    self.CP("dve", rowB[:], self.ps[7][0:1, 0:72], ["ps7"], [K("rowB")])
                    self.MEMSET("dve", rowS[:, 0:4], 0.0, [K("rowS")])
                    for hh in range(4):
                        Mv = rowM[:].rearrange("p (c h) -> p c h", h=4)[:, :, hh]
                        Bv = rowB[:].rearrange("p (c h) -> p c h", h=4)[:, :, hh]
                        Sv = rowS[:, 4:76].rearrange("p (c h) -> p c h", h=4)[:, :, hh]
                        P.op("dve", lambda e, Mv=Mv, Bv=Bv, Sv=Sv: e.tensor_tensor_scan(out=Sv, data0=Mv, data1=Bv, initial=0.0, op0=ALU.max, op1=ALU.add),
                             r=[K("rowM"), K("rowB")], w=[K("rowS")])
                    self.TT("dve", rowMM[:], rowM[:], rowS[:, 0:72], ALU.max, [K("rowM"), K("rowS")], [K("rowMM")])
                    self.TT("dve", rowA[:], rowS[:, 0:72], rowMM[:], ALU.subtract, [K("rowS"), K("rowMM")], [K("rowA")])
                    self.ACT(rowA[:], rowA[:], AF.Exp, [K("rowA")], [K("rowA")])
                    self.MM(self.ps[6][0:72, 0:1], "ps6", [(rowMM[:], self.onesf[0:1, 0:1])], [K("rowMM"), "onesf"])
                    self.CP("dve", mmc[:], self.ps[6][0:72, 0:1], ["ps6"], [K("mmc")])
                    self.MM(self.ps[7][:, 0:72], "ps7", [(self.onesf[0:1, :], rowA[:])], [K("rowA"), "onesf"])
                    self.CP("dve", ATAB[:, d, :], self.ps[7][:, 0:72], ["ps7"], [K("ATAB")])
                    self.TS("dve", ee[:], dd[:], mmc[:, 0:1], None, ALU.subtract, None, [K("dd"), K("mmc")], [K("ee")])
                    self.ACT(ee[:], ee[:], AF.Exp, [K("ee")], [K("ee")])
                    self.TS("dve", th[:], cum[:], mmc[:, 0:1], None, ALU.add, None, [K("cum"), K("mmc")], [K("th")])
                    self.ACT(th[:], th[:], AF.Exp, [K("th")], [K("th")], scale=-1.0)
                    self.TR(self.ps[6][:, 0:72], "ps6", ee[:], self.identf[0:72, 0:72], [K("ee"), "identf"])
                    self.CP("dve", ETAB[:, d, :], self.ps[6][:, 0:72], ["ps6"], [K("ETAB")])
                    self.TR(self.ps[7][:, 0:72], "ps7", th[:], self.identf[0:72, 0:72], [K("th"), "identf"])
                    self.CP("dve", TTAB[:, d, :], self.ps[7][:, 0:72], ["ps7"], [K("TTAB")])
                if "gates" in self.debug and b == 0 and l == 0:
                    for nm, t in (("ETAB", ETAB), ("TTAB", TTAB), ("ATAB", ATAB)):
                        o = self.dbg_out(nm, [128, 144])
                        self.DMA("sp", o, t[:].rearrange("p d n -> p (d n)"), self.dbg_sem, r=[K(nm)])
            if h == 0:
                P.barrier()
            for d in range(2):
                self.MEMSET("pool", Cst[d][:], 0.0, [K(f"Cst{d}")])
            done = [0] * NT
            orders = [list(range(NT)), order_b]
            masks = [self.maskf, self.maskb]
            mkeys = ["maskf", "maskb"]

            def banks(d, s_):
                pb = d * 4
                return pb, pb + 1, pb + 2 + (s_ % 2)

            def part1(d, s_):
                c = orders[d][s_]
                r = s_ * 4 + h
                e_col = ETAB[:, d, r:r + 1]
                tok = slice(c * 128, (c + 1) * 128)
                tb, sb, nb = banks(d, s_)
                ke, pm = Ke2[d][s_ % 2], Pm2[d][s_ % 2]
                kek, pmk = K(f"Ke{d}{s_ % 2}"), K(f"Pm{d}{s_ % 2}")
                psb = self.ps[tb][:].bitcast(BF16)
                self.TR(psb[:, 0:128], f"ps{tb}", kT[:, tok], self.identb[:], [K("kT"), "identb"])
                self.ACT(ke[:], psb[:, 0:128], AF.Copy, [f"ps{tb}", K("ETAB")], [kek], scale=e_col)
                self.MM(self.ps[sb][:, 0:128], f"ps{sb}", [(kT[:, tok], qT[:, tok])], [K("kT"), K("qT")])
                self.STT(pm[:], self.ps[sb][:, 0:128], e_col, masks[d][:], ALU.mult, ALU.mult, [f"ps{sb}", K("ETAB"), mkeys[d]], [pmk])

            def part1b(d, s_):
                c = orders[d][s_]
                tb, sb, nb = banks(d, s_)
                pm = Pm2[d][s_ % 2]
                pmk = K(f"Pm{d}{s_ % 2}")
                self.MM(self.ps[nb][:, 0:129], f"ps{nb}", [(pm[:], Va[:, c, :])], [pmk, K("Va")], start=True, stop=False)

            def cop(d, s_):
                r = s_ * 4 + h
                a_col = ATAB[:, d, r:r + 1]
                self.TS("pool", Cop[d][:], Cst[d][:], a_col, None, ALU.mult, None, [K(f"Cst{d}"), K("ATAB")], [K(f"Cop{d}")])

            def part2(d, s_):
                c = orders[d][s_]
                r = s_ * 4 + h
                t_col = TTAB[:, d, r:r + 1]
                a_col = ATAB[:, d, r:r + 1]
                tok = slice(c * 128, (c + 1) * 128)
                tb, sb, nb = banks(d, s_)
                ke = Ke2[d][s_ % 2]
                kek = K(f"Ke{d}{s_ % 2}")
                pn = self.ps[nb]
                self.MM(pn[:, 0:129], f"ps{nb}", [(qT[:, tok], Cop[d][:])], [K("qT"), K(f"Cop{d}")], start=False, stop=True)
                pst = self.ps[tb]
                self.MM(pst[:, 0:129], f"ps{tb}", [(ke[:], Va[:, c, :])], [kek, K("Va")])
                self.STT(Cst[d][:], Cst[d][:], a_col, pst[:, 0:129], ALU.mult, ALU.add, [K(f"Cst{d}"), K("ATAB"), f"ps{tb}"], [K(f"Cst{d}")])
                self.CP("act", Hd[d][:, s_, :], pn[:, 0:129], [f"ps{nb}"], [K(f"Hd{d}_{s_}")])

            for d in range(2):
                part1(d, 0)
            for d in range(2):
                part1b(d, 0)
            for step in range(NT):
                if step + 1 < NT:
                    for d in range(2):
                        part1(d, step + 1)
                        part1b(d, step + 1)
                for d in range(2):
                    cop(d, step)
                    part2(d, step)
            for d in range(2):
                dk_all = [K(f"Hd{d}_{i}") for i in range(NT)]
                dnv = Hd[d][:, :, 128]
                thv = TTAB[:, d, :].rearrange("p (c h) -> p c h", h=4)[:, :, h]
                self.TT("dve", rin[:, d, :], dnv, thv, ALU.max, dk_all + [K("TTAB")], [K("rin")])
                self.STT(rin[:, d, :], dnv, -1.0, rin[:, d, :], ALU.mult, ALU.max, dk_all + [K("rin")], [K("rin")])
                P.op("dve", lambda e, d=d: e.reciprocal(out=rin[:, d, :], in_=rin[:, d, :]), r=[K("rin")], w=[K("rin")])
            Hs = Hd[0]
            for c in range(NT):
                pf_, pb_ = pos_f[c], pos_b[c]
                self.TS("pool", Hd[1][:, pb_, 0:128], Hd[1][:, pb_, 0:128], rin[:, 1, pb_:pb_ + 1], None, ALU.mult, None,
                        [K(f"Hd1_{pb_}"), K("rin")], [K(f"Hd1_{pb_}")])
                self.STT(Hs[:, c, 0:128], Hs[:, c, 0:128], rin[:, 0, pf_:pf_ + 1], Hd[1][:, pb_, 0:128], ALU.mult, ALU.add,
                         [K(f"Hd0_{c}"), K(f"Hd1_{pb_}"), K("rin")], [K(f"Hd0_{c}")])
            for c in range(NT):
                P.op("dve", lambda e, c=c: e.bn_stats(out=st6a[:, c, :], in_=Hs[:, c, 0:128]), r=[K(f"Hd0_{c}")], w=[K("st6a")])
                P.op("dve", lambda e, c=c: e.bn_aggr(out=mva[:, c, :], in_=st6a[:, c, :]), r=[K("st6a")], w=[K("mva")])
            self.ACT(rsa[:], mva[:, :, 1], AF.Sqrt, [K("mva")], [K("rsa")], bias=self.epsc[:, 0:1])
            P.op("dve", lambda e: e.reciprocal(out=rsa[:], in_=rsa[:]), r=[K("rsa")], w=[K("rsa")])
            groups = [(0, 2), (2, 4), (6, 4), (10, 4), (14, 4)]
            for gi, (c0, ncg) in enumerate(groups):
                hk = [K(f"Hd0_{c}") for c in range(c0, c0 + ncg)]
                hv = Hs[:, c0:c0 + ncg, 0:128]
                self.TT("dve", hv, hv, mva[:, c0:c0 + ncg, 0:1].to_broadcast([128, ncg, 128]), ALU.subtract, hk + [K("mva")], hk)
                self.TT("pool", hv, hv, rsa[:, c0:c0 + ncg].unsqueeze(2).to_broadcast([128, ncg, 128]), ALU.mult, hk + [K("rsa")], hk)
                pf = self.ps[gi % 4]
                for i in range(ncg):
                    self.TR(pf[:, i * 128:(i + 1) * 128], f"ps{gi % 4}", Hs[:, c0 + i, 0:128], self.identf[:], hk + ["identf"])
                tsl = slice(c0 * 128, (c0 + ncg) * 128)
                self.STT(hmT[:, h, tsl], pf[:, 0:ncg * 128], mlg[:, h:h + 1], sgo[:, tsl], ALU.mult, ALU.mult,
                         [f"ps{gi % 4}", f"vec{l}", K("sgo")], [f"hmT{c}" for c in range(c0, c0 + ncg)])
        if "hmT" in self.debug and b == 0 and l == 0:
            o = self.dbg_out("hmT", [128, 4 * TALL], BF16)
            self.DMA("sp", o, hmT[:].rearrange("p a n -> p (a n)"), self.dbg_sem, r=tkeys("hmT", 0, TALL))
        P.phase_end()

    def phase_D1(self, b, l, last):
        P = self.P
        P.phase_begin()
        uid = P.uid
        K = lambda n: f"{n}_p{uid}"
        uT = P.psbuf("uT", [128, 8, 512], BF16)
        accm = P.psbuf("accm", [128, 8, 512], F32)
        mT = P.psbuf("mT", [128, 8, 512], BF16)
        sg = P.psbuf("sg", [128, 512], F32)
        tt_ = P.psbuf("tt", [128, 512], F32)
        tmp = dict(sq=P.psbuf("lsq", [128, 512], F32), mean=P.psbuf("lmean", [128, 512], F32),
                   rstd=P.psbuf("lrstd", [128, 512], F32), t1=P.psbuf("lt1", [128, 512], F32))
        ring = self.make_ring("rd", 2, 12 * 512)
        srcs = [self.brT[1], self.brT[2], self.brT[0]]
        skeys = ["hcT", "hmT", "oT"]
        b_g = self.V(l, "b_g", 0, 24)
        chunks = ([(0, 256, 1)] if not last else []) + [(256 + 512 * i, 512, 0) for i in range(4)]
        def do_ln1(c):
            if c is None:
                return
            m0, mln = c
            zs = [self.xT[:, fc, m0:m0 + mln] for fc in range(8)]
            xk = tkeys("xT", m0, mln)
            self.fm_layernorm(zs, [xk] * 8, zs, [xk] * 8, mln, [self.V(l, "ln1_g", i) for i in range(8)],
                              [self.V(l, "ln1_b", i) for i in range(8)], 1024.0, tmp)

        pending_ln = None
        for (n0, ln, who) in chunks:
            self.make_uT(uT, K("uT"), n0, ln, who, 0)
            for br in range(3):
                for g in range(2):
                    wt, wkey = self.ring_load(ring, l, "merge", br * 2 + g, 12 * 512)
                    w3 = wt[:].rearrange("p (k n) -> p k n", n=512)
                    for q in range(4):
                        fc = g * 4 + q
                        py, pg = self.ps[(2 * q) % 6], self.ps[(2 * q + 1) % 6]
                        ky, kg = f"ps{(2 * q) % 6}", f"ps{(2 * q + 1) % 6}"
                        self.MM(py[:, 0:ln], ky, [(w3[:, k, q * 128:(q + 1) * 128], srcs[br][:, k, n0:n0 + ln]) for k in range(4)],
                                [wkey] + tkeys(skeys[br], n0, ln))
                        self.MM(pg[:, 0:ln], kg, [(w3[:, 4 + k, q * 128:(q + 1) * 128], uT[:, k, 0:ln]) for k in range(8)], [wkey, K("uT")])
                        self.ACT(sg[:, 0:ln], pg[:, 0:ln], AF.Sigmoid, [kg, f"vec{l}"], [K("sg")], bias=b_g[:, br * 8 + fc:br * 8 + fc + 1])
                        if br == 0:
                            self.TT("dve", accm[:, fc, 0:ln], py[:, 0:ln], sg[:, 0:ln], ALU.mult, [ky, K("sg")], [K(f"accm{fc}")])
                        else:
                            self.TT("dve", tt_[:, 0:ln], py[:, 0:ln], sg[:, 0:ln], ALU.mult, [ky, K("sg")], [K("tt")])
                            if br == 1:
                                self.TT("pool", accm[:, fc, 0:ln], accm[:, fc, 0:ln], tt_[:, 0:ln], ALU.add, [K("tt"), K(f"accm{fc}")], [K(f"accm{fc}")])
                            else:
                                self.TT("pool", mT[:, fc, 0:ln], accm[:, fc, 0:ln], tt_[:, 0:ln], ALU.add, [K("tt"), K(f"accm{fc}")], [K("mT")])
            do_ln1(pending_ln)
            pending_ln = None
            for s in range(2):
                wt, wkey = self.ring_load(ring, l, "wout", s, 8 * 512)
                w3 = wt[:, 0:8 * 512].rearrange("p (k n) -> p k n", n=512)
                for q in range(4):
                    fc = s * 4 + q
                    pi = fc % 6
                    ps = self.ps[pi]
                    self.MM(ps[:, 0:ln], f"ps{pi}", [(w3[:, k, q * 128:(q + 1) * 128], mT[:, k, 0:ln]) for k in range(8)], [wkey, K("mT")])
                    wc = self.cur_b if who == 0 else 2
                    g1 = self.modT[:, 16 + fc, wc:wc + 1]
                    bg1 = self.modP[:, 16 + fc, wc:wc + 1]
                    self.TS("dve", tt_[:, 0:ln], ps[:, 0:ln], g1, bg1, ALU.mult, ALU.add, [f"ps{pi}"] + list(self.mk), [K("tt")])
                    xs = self.xT[:, fc, n0:n0 + ln]
                    self.STT(xs, xs, ALPHA, tt_[:, 0:ln], ALU.mult, ALU.add, tkeys("xT", n0, ln) + [K("tt")], tkeys("xT", n0, ln))
            pending_ln = (n0, ln)
        do_ln1(pending_ln)
        if "x1" in self.debug and b == 0 and l == 0:
            o = self.dbg_out("x1T", [128, 8 * TALL])
            self.DMA("sp", o, self.xT[:].rearrange("p a n -> p (a n)"), self.dbg_sem, r=tkeys("xT", 0, TALL))
        P.phase_end()

    def phase_D2(self, b, l, last):
        P = self.P
        P.phase_begin()
        uid = P.uid
        K = lambda n: f"{n}_p{uid}"
        xm = P.psbuf("xm", [128, 8, 512], BF16)
        hid = P.psbuf("hid", [128, 32, 512], BF16)
        tt_ = P.psbuf("tt", [128, 512], F32)
        tmp = dict(sq=P.psbuf("lsq", [128, 512], F32), mean=P.psbuf("lmean", [128, 512], F32),
                   rstd=P.psbuf("lrstd", [128, 512], F32), t1=P.psbuf("lt1", [128, 512], F32))
        ring = self.make_ring("re", 2, 32 * 256)
        b1 = self.V(l, "b1", 0, 32)
        chunks = ([(0, 256, 1)] if not last else []) + [(256 + 512 * i, 512, 0) for i in range(4)]
        def do_ln2(c):
            if c is None:
                return
            m0, mln = c
            zs = [self.xT[:, fc, m0:m0 + mln] for fc in range(8)]
            xk = tkeys("xT", m0, mln)
            self.fm_layernorm(zs, [xk] * 8, zs, [xk] * 8, mln, [self.V(l, "ln2_g", i) for i in range(8)],
                              [self.V(l, "ln2_b", i) for i in range(8)], 1024.0, tmp)

        pending_ln = None
        for (n0, ln, who) in chunks:
            self.make_uT(xm, K("xm"), n0, ln, who, 3)
            for s in range(8):
                wt, wkey = self.ring_load(ring, l, "w1", s, 8 * 512)
                w3 = wt[:, 0:8 * 512].rearrange("p (k n) -> p k n", n=512)
                for q in range(4):
                    j = s * 4 + q
                    pi = j % 4
                    ps = self.ps[pi]
                    self.MM(ps[:, 0:ln], f"ps{pi}", [(w3[:, k, q * 128:(q + 1) * 128], xm[:, k, 0:ln]) for k in range(8)], [wkey, K("xm")])
                    self.TS("dve", tt_[:, 0:ln], ps[:, 0:ln], b1[:, j:j + 1], 0.0, ALU.add, ALU.max, [f"ps{pi}", f"vec{l}"], [K("tt")])
                    self.ACT(hid[:, j, 0:ln], tt_[:, 0:ln], AF.Square, [K("tt")], [K("hid")])
            do_ln2(pending_ln)
            pending_ln = None
            for s in range(4):
                wt, wkey = self.ring_load(ring, l, "w2", s, 32 * 256)
                w3 = wt[:].rearrange("p (k n) -> p k n", n=256)
                for q in range(2):
                    fc = s * 2 + q
                    pi = 4 + fc % 2
                    ps = self.ps[pi]
                    self.MM(ps[:, 0:ln], f"ps{pi}", [(w3[:, j, q * 128:(q + 1) * 128], hid[:, j, 0:ln]) for j in range(32)], [wkey, K("hid")])
                    wc = self.cur_b if who == 0 else 2
                    g2 = self.modT[:, 40 + fc, wc:wc + 1]
                    bg2 = self.modP[:, 24 + fc, wc:wc + 1]
                    self.TS("dve", tt_[:, 0:ln], ps[:, 0:ln], g2, bg2, ALU.mult, ALU.add, [f"ps{pi}"] + list(self.mk), [K("tt")])
                    xs = self.xT[:, fc, n0:n0 + ln]
                    self.STT(xs, xs, ALPHA, tt_[:, 0:ln], ALU.mult, ALU.add, tkeys("xT", n0, ln) + [K("tt")], tkeys("xT", n0, ln))
            pending_ln = (n0, ln)
        do_ln2(pending_ln)
        P.phase_end()


def _small_consts(bld):
    P = bld.P
    bld.epsc = P.sbuf("epsc", [128, 1], F32)
    bld.onec = P.sbuf("onec", [128, 1], F32)
    bld.MEMSET("pool", bld.epsc[:], EPS, ["epsc"])
    bld.MEMSET("pool", bld.onec[:], 1.0, ["onec"])


_orig_consts = Builder.consts


def _consts(self):
    _orig_consts(self)
    _small_consts(self)


Builder.consts = _consts


def rope_tables():
    pos = np.arange(NLAT)
    rr = (pos // 64).astype(np.float32)
    cc = (pos % 64).astype(np.float32)
    inv = (np.float32(10000.0) ** (-np.arange(8, dtype=np.float32) / np.float32(8))).astype(np.float32)
    C = np.zeros((32, NLAT), np.float32)
    S = np.zeros((32, NLAT), np.float32)
    for a, p in enumerate((rr, cc)):
        ang = (p[None, :] * inv[:, None]).astype(np.float32)
        for j in range(2):
            C[a * 16 + j * 8:a * 16 + j * 8 + 8] = np.cos(ang)
            S[a * 16 + j * 8:a * 16 + j * 8 + 8] = np.sin(ang) * (-1.0 if j == 0 else 1.0)
    return C, S


_NC_CACHE = {}


def kernel(**inputs):
    if "nc" not in _NC_CACHE:
        _NC_CACHE["nc"] = Builder().build()
    nc = _NC_CACHE["nc"]
    C, S = rope_tables()
    names = ["w_mod", "b_mod", "w_in", "b_in", "w_dw", "b_dw", "conv_norm_g", "conv_norm_b", "w_conv_out", "mlstm_norm_g",
             "w_mlstm_out", "q_norm_g", "w_uq", "kv_norm_g", "w_ukv", "w_mla_out", "w_out", "b_out", "ln1_g", "ln1_b",
             "w1", "b1", "w2", "b2", "ln2_g", "ln2_b", "c_ctx"]
    shared = {n: np.ascontiguousarray(np.asarray(inputs[n], dtype=np.float32)) for n in names}
    shared["rope_c"] = C
    shared["rope_s"] = S
    x = np.asarray(inputs["x"], dtype=np.float32)
    c = np.asarray(inputs["c"], dtype=np.float32)
    ctx = np.asarray(inputs["ctx"], dtype=np.float32)
    in_maps = []
    for i in range(8):
        m = dict(shared)
        m["x"] = np.ascontiguousarray(x[i * NB:(i + 1) * NB])
        m["c"] = np.ascontiguousarray(c[i * NB:(i + 1) * NB])
        m["ctx"] = np.ascontiguousarray(ctx[i * NB:(i + 1) * NB])
        in_maps.append(m)
    res = run_bass_kernel_spmd(nc, in_maps, core_ids=list(range(8)))
    return np.concatenate([np.asarray(r["out"], dtype=np.float32) for r in res.results], axis=0)
```

```python
import time
import numpy as np
import concourse.bass as bass
import concourse.mybir as mybir
from concourse.bass_utils import run_bass_kernel_spmd
from contextlib import ExitStack

F32 = mybir.dt.float32
BF16 = mybir.dt.bfloat16
ALU = mybir.AluOpType
AF = mybir.ActivationFunctionType
AX = mybir.AxisListType

COMPUTE = ("pe", "act", "dve", "pool")
ALLENG = ("pe", "act", "dve", "pool", "sp")

D = 1024
NCTX = 256
NLAT = 2048
TALL = NCTX + NLAT
NT = TALL // 128
DEPTH = 2
NB = 2
ALPHA = float((2 * DEPTH) ** 0.25)
EPS = 1e-5
N_IN = 7216
O_A, O_Q, O_K, O_V, O_O, O_GIF, O_CQ, O_CKV, O_KR, O_G = 0, 1024, 1536, 2048, 2560, 3072, 3088, 3856, 4112, 4144
MLA_SCALE = float(96 ** -0.5)
KSCALE = float(128 ** -0.5)


class DSem:
    def __init__(self, handle, name):
        self.h = handle
        self.count = 0
        self.name = name


class Prog:
    def __init__(self, nc, same_eng_sync=True):
        self.nc = nc
        self.es = ExitStack()
        self.ops = {e: [] for e in ALLENG}
        self.cnt = {e: 0 for e in COMPUTE}
        self.sem = {}
        for e in COMPUTE:
            self.sem[e] = self.es.enter_context(nc.semaphore("s_" + e))
        self.seen = {e: {} for e in ALLENG}
        self.last_w = {}
        self.readers = {}
        self.same_eng_sync = same_eng_sync
        self.dsems = []
        self.uid = 0
        self.phase_es = None
        self.layer_es = None

    def sbuf(self, name, shape, dtype):
        return self.es.enter_context(self.nc.sbuf_tensor(name, list(shape), dtype))

    def psum(self, name, shape, dtype=F32):
        return self.es.enter_context(self.nc.psum_tensor(name, list(shape), dtype))

    def dsem(self, name=None):
        name = name or f"d{len(self.dsems)}"
        d = DSem(self.es.enter_context(self.nc.semaphore(name)), name)
        self.dsems.append(d)
        return d

    def phase_begin(self):
        assert self.phase_es is None
        self.phase_es = ExitStack()
        self.uid += 1

    def psbuf(self, name, shape, dtype):
        return self.phase_es.enter_context(self.nc.sbuf_tensor(f"{name}_p{self.uid}", list(shape), dtype))

    def phase_end(self):
        self.barrier()
        self.phase_es.close()
        self.phase_es = None

    def layer_begin(self):
        assert self.layer_es is None and self.phase_es is None
        self.layer_es = ExitStack()

    def lsbuf(self, name, shape, dtype):
        assert self.phase_es is None
        self.uid += 1
        return self.layer_es.enter_context(self.nc.sbuf_tensor(f"{name}_l{self.uid}", list(shape), dtype))

    def layer_end(self):
        assert self.phase_es is None
        self.layer_es.close()
        self.layer_es = None

    def seal(self, dsem, keys):
        for k in keys:
            self.last_w[k] = (dsem, dsem.count)

    def _deps(self, eng, reads, writes):
        deps = {}

        def add(ev):
            if ev is None:
                return
            k, v = ev
            if deps.get(k, 0) < v:
                deps[k] = v

        for r in reads:
            add(self.last_w.get(r))
        for w in writes:
            add(self.last_w.get(w))
            for ev in self.readers.get(w, ()):
                add(ev)
        out = []
        for k, v in deps.items():
            if isinstance(k, str) and k == eng and (eng == "pe" or not self.same_eng_sync):
                continue
            if self.seen[eng].get(k, 0) >= v:
                continue
            self.seen[eng][k] = v
            out.append((k, v))
        return out

    def _record(self, ev, reads, writes):
        for r in reads:
            lst = self.readers.setdefault(r, [])
            lst[:] = [x for x in lst if x[0] != ev[0]]
            lst.append(ev)
        for w in writes:
            self.last_w[w] = ev
            self.readers[w] = []

    def _h(self, k):
        return self.sem[k] if isinstance(k, str) else k.h

    def op(self, eng, fn, r=(), w=(), inc=True, force_self=False):
        w = list(w) + [k for k in r if k.startswith("ps") and k[2:].isdigit() and k not in w]
        waits = self._deps(eng, r, w)
        if force_self and self.cnt[eng] > 0 and self.seen[eng].get(eng, 0) < self.cnt[eng]:
            self.seen[eng][eng] = self.cnt[eng]
            waits.append((eng, self.cnt[eng]))
        if inc:
            self.cnt[eng] += 1
            ev = (eng, self.cnt[eng])
        else:
            ev = (eng, self.cnt[eng] + 1)
        self._record(ev, r, w)
        self.ops[eng].append((waits, fn, inc, None))
        return ev

    def dma(self, q, fn, dsem, r=(), w=()):
        waits = self._deps(q, r, w)
        dsem.count += 16
        ev = (dsem, dsem.count)
        self._record(ev, r, w)
        self.ops[q].append((waits, fn, False, dsem))
        return ev

    def wait_all(self, eng, evs):
        waits = []
        for k, v in evs:
            if isinstance(k, str) and k == eng:
                continue
            if self.seen[eng].get(k, 0) >= v:
                continue
            self.seen[eng][k] = v
            waits.append((k, v))
        if waits:
            self.ops[eng].append((waits, None, False, None))

    def barrier(self):
        evs = [(e, self.cnt[e]) for e in COMPUTE if self.cnt[e] > 0]
        evs += [(d, d.count) for d in self.dsems if d.count > 0 and not getattr(d, "nobarrier", False)]
        for e in ALLENG:
            self.wait_all(e, evs)

    def emit(self):
        nc = self.nc
        engmap = {"pe": "tensor", "act": "scalar", "dve": "vector", "pool": "gpsimd", "sp": "sync"}
        with nc.Block() as block:
            for e in ALLENG:
                ops = self.ops[e]
                if not ops:
                    continue

                def body(engine, ops=ops, e=e):
                    for waits, fn, inc, dsem in ops:
                        for k, v in waits:
                            engine.wait_ge(self._h(k), v)
                        if fn is None:
                            continue
                        ins = fn(engine)
                        if dsem is not None:
                            ins.then_inc(dsem.h, 16)
                        elif inc:
                            ins.then_inc(self.sem[e], 1)

                getattr(block, engmap[e])(body)

    def close(self):
        self.es.close()


def tkeys(pfx, n0, ln):
    return [f"{pfx}{t}" for t in range(n0 // 128, (n0 + ln + 127) // 128)]


class Builder:
    def __init__(self, nlayers=DEPTH, nb=NB, debug=None, stop_after=None):
        self.nl = nlayers
        self.nb = nb
        self.debug = debug or set()
        self.stop_after = stop_after
        self.nc = bass.Bass("TRN2", target_bir_lowering=False)
        self.P = Prog(self.nc)
        self.dbg_outs = {}

    def MM(self, ps_ap, ps_key, pairs, rkeys, start=True, stop=True, serialize=False):
        n = len(pairs)
        for i, (l, r) in enumerate(pairs):
            last = i == n - 1
            self.P.op("pe", lambda e, l=l, r=r, i=i, last=last: e.matmul(ps_ap, lhsT=l, rhs=r, start=(start and i == 0), stop=(stop and last)),
                      r=rkeys, w=[ps_key], inc=True, force_self=serialize)

    def TR(self, ps_ap, ps_key, in_ap, ident_ap, rkeys):
        self.P.op("pe", lambda e: e.transpose(ps_ap, in_ap, ident_ap), r=rkeys, w=[ps_key])

    def ACT(self, out, in_, func, r, w, bias=None, scale=None):
        kw = {}
        if bias is not None:
            kw["bias"] = bias
        if scale is not None:
            kw["scale"] = scale
        self.P.op("act", lambda e: e.activation(out=out, in_=in_, func=func, **kw), r=r, w=w)

    def TS(self, eng, out, in0, s1, s2, op0, op1, r, w):
        if op1 is None and eng == "pool" and op0 in (ALU.mult, ALU.add):
            op1 = ALU.add if op0 == ALU.mult else ALU.mult
            s2 = 0.0 if op0 == ALU.mult else 1.0
        if op1 is None:
            self.P.op(eng, lambda e: e.tensor_scalar(out=out, in0=in0, scalar1=s1, scalar2=None, op0=op0), r=r, w=w)
        else:
            self.P.op(eng, lambda e: e.tensor_scalar(out=out, in0=in0, scalar1=s1, scalar2=s2, op0=op0, op1=op1), r=r, w=w)

    def TT(self, eng, out, in0, in1, op, r, w):
        self.P.op(eng, lambda e: e.tensor_tensor(out=out, in0=in0, in1=in1, op=op), r=r, w=w)

    def STT(self, out, in0, scalar, in1, op0, op1, r, w):
        self.P.op("dve", lambda e: e.scalar_tensor_tensor(out=out, in0=in0, scalar=scalar, in1=in1, op0=op0, op1=op1), r=r, w=w)

    def CP(self, eng, out, in_, r, w):
        if eng == "act":
            self.P.op("act", lambda e: e.copy(out=out, in_=in_), r=r, w=w)
        else:
            self.P.op(eng, lambda e: e.tensor_copy(out=out, in_=in_), r=r, w=w)

    def MEMSET(self, eng, ap, val, w):
        self.P.op(eng, lambda e: e.memset(ap, val), w=w)

    def DMA(self, q, out, in_, dsem, r=(), w=(), **kw):
        return self.P.dma(q, lambda e: e.dma_start(out=out, in_=in_, **kw), dsem, r=r, w=w)

    def declare(self):
        nc = self.nc
        L = DEPTH

        def inp(name, shape):
            return nc.dram_tensor(name, list(shape), F32, kind="ExternalInput").ap()

        self.x = inp("x", [NB, NLAT, D])
        self.c = inp("c", [NB, D])
        self.ctx = inp("ctx", [NB, NCTX, D])
        self.c_ctx = inp("c_ctx", [D])
        self.w_mod = inp("w_mod", [L, D, 6 * D])
        self.b_mod = inp("b_mod", [L, 6 * D])
        self.w_in = inp("w_in", [L, D, N_IN])
        self.b_in = inp("b_in", [L, N_IN])
        self.w_dw = inp("w_dw", [L, 31, 512])
        self.b_dw = inp("b_dw", [L, 512])
        self.cn_g = inp("conv_norm_g", [L, 512])
        self.cn_b = inp("conv_norm_b", [L, 512])
        self.w_conv_out = inp("w_conv_out", [L, 512, D])
        self.ml_g = inp("mlstm_norm_g", [L, 512])
        self.w_ml_out = inp("w_mlstm_out", [L, 512, D])
        self.qn_g = inp("q_norm_g", [L, 768])
        self.w_uq = inp("w_uq", [L, 768, 768])
        self.kvn_g = inp("kv_norm_g", [L, 256])
        self.w_ukv = inp("w_ukv", [L, 256, 1024])
        self.w_mla_out = inp("w_mla_out", [L, 512, D])
        self.w_out = inp("w_out", [L, D, D])
        self.b_out = inp("b_out", [L, D])
        self.ln1_g = inp("ln1_g", [L, D])
        self.ln1_b = inp("ln1_b", [L, D])
        self.w1 = inp("w1", [L, D, 4 * D])
        self.b1 = inp("b1", [L, 4 * D])
        self.w2 = inp("w2", [L, 4 * D, D])
        self.b2 = inp("b2", [L, D])
        self.ln2_g = inp("ln2_g", [L, D])
        self.ln2_b = inp("ln2_b", [L, D])
        self.rope_c = inp("rope_c", [32, NLAT])
        self.rope_s = inp("rope_s", [32, NLAT])
        self.out = nc.dram_tensor("out", [NB, NLAT, D], F32, kind="ExternalOutput").ap()

        def scr(name, shape):
            return nc.dram_tensor(name, list(shape), BF16, kind="Internal").ap()

        self.S = {}
        for l in range(self.nl):
            self.S[l] = dict(
                mod=scr(f"s_mod{l}", [12, 128, 8 * 512]),
                in_kv=scr(f"s_inkv{l}", [1, 128, 8 * 320]),
                ukv=scr(f"s_ukv{l}", [2, 128, 2 * 512]),
                in_cq=scr(f"s_incq{l}", [2, 128, 8 * 384]),
                uq=scr(f"s_uq{l}", [2, 128, 6 * 512]),
                in_a=scr(f"s_ina{l}", [2, 128, 8 * 512]),
                in_gif=scr(f"s_ingif{l}", [1, 128, 8 * 16]),
                in_head=scr(f"s_inhead{l}", [4, 128, 8 * 512]),
                merge=scr(f"s_merge{l}", [6, 128, 12 * 512]),
                wout=scr(f"s_wout{l}", [2, 128, 8 * 512]),
                w1=scr(f"s_w1{l}", [8, 128, 8 * 512]),
                w2=scr(f"s_w2{l}", [4, 128, 32 * 256]),
            )

    def dbg_out(self, name, shape, dtype=F32):
        t = self.nc.dram_tensor("dbg_" + name, list(shape), dtype, kind="ExternalOutput").ap()
        self.dbg_outs[name] = t
        return t

    def precast(self):
        P = self.P
        self.Ssem = {}
        self.pc_hist = []
        self.pending_casts = []
        for l in range(self.nl):
            for name in ("mod", "in_kv", "ukv", "in_cq", "uq", "in_a", "in_gif", "in_head", "merge", "wout", "w1", "w2"):
                self.pending_casts.append((l, name))

    def emit_casts(self, n):
        for _ in range(n):
            if not self.pending_casts:
                return
            l, name = self.pending_casts.pop(0)
            self._emit_cast_group(l, name)

    def ensure_cast(self, l, name):
        while (l, name) in self.pending_casts:
            self.emit_casts(1)

    def _emit_cast_group(self, l, name):
        P = self.P
        S = self.S[l]
        hist = self.pc_hist
        if len(hist) >= 2:
            pd = hist[-2]
            P.wait_all("pool", [(pd, pd.count)])
        d = P.dsem(f"pc_{name}{l}")
        d.nobarrier = True
        self.Ssem[(l, name)] = d
        hist.append(d)
        key = f"S{l}{name}"

        def cast(dst, src):
            self.DMA("pool", dst, src, d, w=[key])

        def v3(ap2d, kt, n):
            return ap2d.rearrange("p (k n) -> p k n", n=n)

        def rows(w2d, c0, n):
            return w2d.rearrange("(k p) n -> p k n", p=128)[:, :, c0:c0 + n]

        if name == "mod":
            for s_ in range(12):
                cast(v3(S["mod"][s_], 8, 512), rows(self.w_mod[l], s_ * 512, 512))
        elif name == "in_kv":
            dst = v3(S["in_kv"][0], 8, 320)
            cast(dst[:, :, 0:256], rows(self.w_in[l], O_CKV, 256))
            cast(dst[:, :, 256:288], rows(self.w_in[l], O_KR, 32))
        elif name == "ukv":
            src4 = self.w_ukv[l].rearrange("(k p) (h e) -> p k h e", p=128, e=128)
            for k in range(2):
                cast(v3(S["ukv"][0], 2, 512)[:, k, :].rearrange("p (h e) -> p h e", e=64), src4[:, k, :, 0:64])
                cast(v3(S["ukv"][1], 2, 512)[:, k, :].rearrange("p (h e) -> p h e", e=64), src4[:, k, :, 64:128])
        elif name == "in_cq":
            for s_ in range(2):
                cast(v3(S["in_cq"][s_], 8, 384), rows(self.w_in[l], O_CQ + s_ * 384, 384))
        elif name == "uq":
            src4 = self.w_uq[l].rearrange("(k p) (h e) -> p k h e", p=128, e=96)
            for k in range(6):
                cast(v3(S["uq"][0], 6, 512)[:, k, :].rearrange("p (h e) -> p h e", e=64), src4[:, k, :, 0:64])
                d1 = v3(S["uq"][1], 6, 512)[:, k, :]
                cast(d1[:, 0:256].rearrange("p (h e) -> p h e", e=32), src4[:, k, :, 64:96])
        elif name == "in_a":
            for s_ in range(2):
                dst = v3(S["in_a"][s_], 8, 512)
                cast(dst[:, :, 0:256], rows(self.w_in[l], O_A + s_ * 256, 256))
                cast(dst[:, :, 256:512], rows(self.w_in[l], O_A + 512 + s_ * 256, 256))
        elif name == "in_gif":
            cast(v3(S["in_gif"][0], 8, 16), rows(self.w_in[l], O_GIF, 16))
        elif name == "in_head":
            for h in range(4):
                dst = v3(S["in_head"][h], 8, 512)
                for i, o in enumerate((O_Q, O_K, O_V, O_O)):
                    cast(dst[:, :, i * 128:(i + 1) * 128], rows(self.w_in[l], o + h * 128, 128))
        elif name == "merge":
            for br, wb in enumerate((self.w_conv_out, self.w_ml_out, self.w_mla_out)):
                for g in range(2):
                    dst = v3(S["merge"][br * 2 + g], 12, 512)
                    cast(dst[:, 0:4, :], rows(wb[l], g * 512, 512))
                    cast(dst[:, 4:12, :], rows(self.w_in[l], O_G + br * 1024 + g * 512, 512))
        elif name == "wout":
            for s_ in range(2):
                cast(v3(S["wout"][s_], 8, 512), rows(self.w_out[l], s_ * 512, 512))
        elif name == "w1":
            for s_ in range(8):
                cast(v3(S["w1"][s_], 8, 512), rows(self.w1[l], s_ * 512, 512))
        elif name == "w2":
            for s_ in range(4):
                cast(v3(S["w2"][s_], 32, 256), rows(self.w2[l], s_ * 256, 256))

    def consts(self):
        P = self.P
        self.identf = P.sbuf("identf", [128, 128], F32)
        self.identb = P.sbuf("identb", [128, 128], BF16)
        self.onesf = P.sbuf("onesf", [128, 128], F32)
        self.onesb = P.sbuf("onesb", [128, 128], BF16)
        self.maskf = P.sbuf("maskf", [128, 128], F32)
        self.maskb = P.sbuf("maskb", [128, 128], F32)
        self.MEMSET("pool", self.identf[:], 0.0, ["identf"])
        P.op("pool", lambda e: e.affine_select(out=self.identf[:], in_=self.identf[:], pattern=[[-1, 128]], compare_op=ALU.not_equal,
                                                fill=1.0, base=0, channel_multiplier=1), r=["identf"], w=["identf"])
        self.CP("pool", self.identb[:], self.identf[:], ["identf"], ["identb"])
        self.MEMSET("pool", self.onesf[:], 1.0, ["onesf"])
        self.MEMSET("pool", self.onesb[:], 1.0, ["onesb"])
        self.MEMSET("pool", self.maskf[:], 1.0, ["maskf"])
        P.op("pool", lambda e: e.affine_select(out=self.maskf[:], in_=self.maskf[:], pattern=[[1, 128]], compare_op=ALU.is_ge,
                                                fill=0.0, base=0, channel_multiplier=-1), r=["maskf"], w=["maskf"])
        self.MEMSET("pool", self.maskb[:], 1.0, ["maskb"])
        P.op("pool", lambda e: e.affine_select(out=self.maskb[:], in_=self.maskb[:], pattern=[[-1, 128]], compare_op=ALU.is_ge,
                                                fill=0.0, base=0, channel_multiplier=1), r=["maskb"], w=["maskb"])
        self.ps = [P.psum(f"ps{i}", [128, 512]) for i in range(8)]
        self.setup_sem = P.dsem("setup")
        self.misc_sem = P.dsem("misc")
        self.dbg_sem = P.dsem("dbg")
        self.stg_sems = [P.dsem(f"stg{i}") for i in range(2)]
        self.ost_sems = [P.dsem(f"ost{i}") for i in range(2)]

    def load_vecs(self):
        P = self.P
        self.VEC = {}
        self.VC = {}
        self.gifb = {}
        allsegs = {}
        for l in range(self.nl):
            segs = []
            bi = self.b_in[l]
            segs.append(("b_a", bi[O_A:O_A + 1024], 1024))
            segs.append(("b_qkv", bi[O_Q:O_Q + 1536], 1536))
            segs.append(("b_o", bi[O_O:O_O + 512], 512))
            segs.append(("b_cq", bi[O_CQ:O_CQ + 768], 768))
            segs.append(("b_ckv", bi[O_CKV:O_CKV + 256], 256))
            segs.append(("b_kr", bi[O_KR:O_KR + 32], 32))
            segs.append(("b_krsw", None, 32))
            segs.append(("b_g", bi[O_G:O_G + 3072], 3072))
            segs.append(("b_mod", self.b_mod[l], 6144))
            segs.append(("b1", self.b1[l], 4096))
            for nm, t in (("b_out", self.b_out), ("b2", self.b2), ("ln1_g", self.ln1_g), ("ln1_b", self.ln1_b),
                          ("ln2_g", self.ln2_g), ("ln2_b", self.ln2_b)):
                segs.append((nm, t[l], 1024))
            segs.append(("qn_g", self.qn_g[l], 768))
            segs.append(("kvn_g", self.kvn_g[l], 256))
            segs.append(("cn_g", self.cn_g[l], 512))
            segs.append(("cn_b", self.cn_b[l], 512))
            segs.append(("b_dw", self.b_dw[l], 512))
            segs.append(("ml_g", self.ml_g[l], 512))
            segs.append(("w_dw", None, 31 * 512))
            allsegs[l] = segs
            nrows = sum((n + 127) // 128 for _, _, n in segs)
            ngrp = (nrows + 127) // 128
            self.VEC[l] = P.sbuf(f"vec{l}", [128, ngrp * 128], F32)
            self.gifb[l] = P.sbuf(f"gifb{l}", [128, 16], F32)
        P.phase_begin()
        d = self.setup_sem
        todo = []
        for l in range(self.nl):
            segs = allsegs[l]
            bi = self.b_in[l]
            nrows = sum((n + 127) // 128 for _, _, n in segs)
            ngrp = (nrows + 127) // 128
            raw = [P.psbuf(f"raw{l}_{g}", [128, 128], F32) for g in range(ngrp)]
            rkeys = [f"raw{l}_{g}" for g in range(ngrp)]
            for g in range(ngrp):
                self.MEMSET("pool", raw[g][:], 0.0, [rkeys[g]])
            cols = {}
            r = 0
            for name, ap1d, n in segs:
                nr = (n + 127) // 128
                cols[name] = r
                rr = 0
                while rr < nr:
                    g, off = divmod(r + rr, 128)
                    take = min(nr - rr, 128 - off)
                    dst = raw[g][off:off + take, :]
                    key = rkeys[g]
                    if name == "w_dw":
                        src = self.w_dw[l].rearrange("k (b p) -> (k b) p", p=128)[rr:rr + take, :]
                        self.DMA("sp", dst, src, d, w=[key])
                    elif name == "b_krsw":
                        for rep in (0, 64):
                            for a in range(2):
                                for j in range(2):
                                    o = rep + a * 16 + j * 8
                                    so = O_KR + a * 16 + (1 - j) * 8
                                    self.DMA("sp", raw[g][off:off + 1, o:o + 8], bi[so:so + 8].unsqueeze(0), d, w=[key])
                    elif name == "b_kr":
                        for rep in (0, 64):
                            self.DMA("sp", raw[g][off:off + 1, rep:rep + n], ap1d.unsqueeze(0), d, w=[key])
                    elif n < 128:
                        self.DMA("sp", raw[g][off:off + 1, 0:n], ap1d.unsqueeze(0), d, w=[key])
                    else:
                        src = ap1d.rearrange("(c p) -> c p", p=128)[rr:rr + take, :]
                        self.DMA("sp", dst, src, d, w=[key])
                    rr += take
                r += nr
            self.VC[l] = cols
            self.DMA("sp", self.gifb[l][:], bi[O_GIF:O_GIF + 16].partition_broadcast(128), d, w=[f"gifb{l}"])
            todo.append((l, raw, rkeys, nrows, ngrp))
        allkeys = [k for (_, _, rk, _, _) in todo for k in rk] + [f"gifb{l}" for l in range(self.nl)]
        P.seal(d, allkeys)
        for (l, raw, rkeys, nrows, ngrp) in todo:
            for g in range(ngrp):
                n_g = min(128, nrows - g * 128)
                ps = self.ps[g % 8]
                self.TR(ps[:, 0:n_g], f"ps{g % 8}", raw[g][0:n_g, :], self.identf[0:n_g, 0:n_g], [rkeys[g], "identf"])
                self.CP("dve", self.VEC[l][:, g * 128:g * 128 + n_g], ps[:, 0:n_g], [f"ps{g % 8}"], [f"vec{l}"])
        P.phase_end()

    def V(self, l, name, j=0, n=1):
        c = self.VC[l][name] + j
        return self.VEC[l][:, c:c + n]

    def make_ring(self, name, nslots, elems):
        P = self.P
        ring = dict(name=name, n=nslots, i=0,
                    tiles=[P.psbuf(f"{name}{i}", [128, elems], BF16) for i in range(nslots)],
                    keys=[f"{name}{i}_p{P.uid}" for i in range(nslots)])
        if not hasattr(self, "ring_sems"):
            self.ring_sems = [P.dsem(f"ring{i}") for i in range(4)]
        return ring

    def ring_load(self, ring, l, grp, slab, nelem):
        self.ensure_cast(l, grp)
        s = ring["i"] % ring["n"]
        ring["i"] += 1
        t = ring["tiles"][s]
        key = ring["keys"][s]
        self.DMA("sp", t[:, 0:nelem], self.S[l][grp][slab], self.ring_sems[s], r=[f"S{l}{grp}"], w=[key])
        return t, key

    def fm_layernorm(self, zs, zkeys, outs, outkeys, ln, g_cols, b_cols, nfeat, tmp, func=AF.Identity, psA=6, psB=7):
        P = self.P
        nch = len(zs)
        sq, mean, rstd, t1 = tmp["sq"], tmp["mean"], tmp["rstd"], tmp["t1"]
        psa, psb = self.ps[psA], self.ps[psB]
        for i in range(nch):
            self.MM(psa[:, 0:ln], f"ps{psA}", [(self.onesf[:], zs[i])], ["onesf"] + zkeys[i], start=(i == 0), stop=(i == nch - 1))
        for i in range(nch):
            self.ACT(sq[:, 0:ln], zs[i], AF.Square, zkeys[i], ["ln_sq"])
            self.MM(psb[:, 0:ln], f"ps{psB}", [(self.onesf[:], sq[:, 0:ln])], ["onesf", "ln_sq"], start=(i == 0), stop=(i == nch - 1))
        inv = 1.0 / nfeat
        self.ACT(mean[:, 0:ln], psa[:, 0:ln], AF.Copy, [f"ps{psA}"], ["ln_mean"], scale=inv)
        self.TT("dve", t1[:, 0:ln], mean[:, 0:ln], mean[:, 0:ln], ALU.mult, ["ln_mean"], ["ln_t1"])
        self.STT(rstd[:, 0:ln], psb[:, 0:ln], inv, t1[:, 0:ln], ALU.mult, ALU.subtract, [f"ps{psB}", "ln_t1"], ["ln_rstd"])
        self.TS("dve", rstd[:, 0:ln], rstd[:, 0:ln], EPS, None, ALU.add, None, ["ln_rstd"], ["ln_rstd"])
        self.ACT(rstd[:, 0:ln], rstd[:, 0:ln], AF.Sqrt, ["ln_rstd"], ["ln_rstd"])
        P.op("dve", lambda e: e.reciprocal(out=rstd[:, 0:ln], in_=rstd[:, 0:ln]), r=["ln_rstd"], w=["ln_rstd"])
        for i in range(nch):
            self.TT("dve", t1[:, 0:ln], zs[i], mean[:, 0:ln], ALU.subtract, zkeys[i] + ["ln_mean"], ["ln_t1"])
            self.TT("pool", t1[:, 0:ln], t1[:, 0:ln], rstd[:, 0:ln], ALU.mult, ["ln_t1", "ln_rstd"], ["ln_t1"])
            self.ACT(outs[i], t1[:, 0:ln], func, ["ln_t1"], outkeys[i], bias=b_cols[i], scale=g_cols[i])

    def make_uT(self, uT, ukey, n0, ln, who, vidx):
        self.emit_casts(1)
        engs = ("act", "dve", "pool", "act", "dve", "act", "dve", "pool")
        for fc in range(8):
            wc = self.cur_b if who == 0 else 2
            sh = self.modT[:, vidx * 8 + fc, wc:wc + 1]
            scp = self.modP[:, (vidx // 3) * 8 + fc, wc:wc + 1]
            rk = tkeys("xT", n0, ln) + list(self.mk)
            if engs[fc] == "act":
                self.ACT(uT[:, fc, 0:ln], self.xT[:, fc, n0:n0 + ln], AF.Identity, rk, [ukey], bias=sh, scale=scp)
            else:
                self.TS(engs[fc], uT[:, fc, 0:ln], self.xT[:, fc, n0:n0 + ln], scp, sh, ALU.mult, ALU.add, rk, [ukey])

    def build(self):
        P = self.P
        self.declare()
        self.consts()
        self.precast()
        self.pending_casts.remove((0, "mod"))
        self.pending_casts.insert(self.pending_casts.index((0, "w2")) + 1, (0, "mod"))
        self.emit_casts(2)
        self.load_vecs()
        self.emit_casts(2)
        self.xT = P.sbuf("xT", [128, 8, TALL], F32)
        self.modTs = [P.sbuf(f"modT{l}", [128, 48, 3], F32) for l in range(self.nl)]
        self.modPs = [P.sbuf(f"modP{l}", [128, 32, 3], F32) for l in range(self.nl)]
        self.brT = [None, None, None]
        self.out_evs = []
        for b in range(self.nb):
            self.load_x(b)
            for l in range(self.nl):
                last = l == DEPTH - 1
                self.cur_b, self.cur_l = b, l
                self.modT, self.modP = self.modTs[l], self.modPs[l]
                self.mk = (f"modT{l}", f"modP{l}")
                if b == 0:
                    self.mod_vectors(b, l)
                if self.stop_after == "mod":
                    break
                P.layer_begin()
                stop = False
                for i, (nm, fn) in enumerate((("oT", self.phase_C), ("hcT", self.phase_A), ("hmT", self.phase_B))):
                    self.brT[i] = P.lsbuf(nm, [128, 4, TALL], BF16)
                    fn(b, l, last)
                    if self.stop_after == "CAB"[i]:
                        stop = True
                        break
                if not stop:
                    self.phase_D1(b, l, last)
                P.layer_end()
                if stop or self.stop_after == "D1":
                    break
                self.phase_D2(b, l, last)
            self.store_x(b)
        P.barrier()
        P.wait_all("sp", self.out_evs[-1:])
        P.emit()
        P.close()
        return self.nc

    def load_x(self, b):
        P = self.P
        P.phase_begin()
        st = [P.psbuf(f"xst{i}", [128, D], F32) for i in range(2)]
        for t in range(NT):
            s = st[t % 2]
            key = f"xst{t % 2}_p{P.uid}"
            src = self.ctx[b, t * 128:(t + 1) * 128, :] if t < 2 else self.x[b, (t - 2) * 128:(t - 1) * 128, :]
            self.DMA("sp", s[:], src, self.stg_sems[t % 2], w=[key])
            for half in range(2):
                pi = (2 * t + half) % 8
                ps = self.ps[pi]
                for q in range(4):
                    fc = half * 4 + q
                    self.TR(ps[:, q * 128:(q + 1) * 128], f"ps{pi}", s[:, fc * 128:(fc + 1) * 128], self.identf[:], [key, "identf"])
                eng = "act" if half == 0 else "dve"
                self.CP(eng, self.xT[:, half * 4:half * 4 + 4, t * 128:(t + 1) * 128],
                        ps[:].rearrange("p (q n) -> p q n", n=128), [f"ps{pi}"], [f"xT{t}"])
        P.phase_end()

    def store_x(self, b):
        P = self.P
        P.phase_begin()
        st = [P.psbuf(f"ost{i}", [128, D], F32) for i in range(2)]
        for t in range(2, NT):
            s = st[t % 2]
            key = f"ost{t % 2}_p{P.uid}"
            for half in range(2):
                pi = (2 * t + half) % 8
                ps = self.ps[pi]
                for q in range(4):
                    fc = half * 4 + q
                    self.TR(ps[:, q * 128:(q + 1) * 128], f"ps{pi}", self.xT[:, fc, t * 128:(t + 1) * 128], self.identf[:], [f"xT{t}", "identf"])
                eng = "act" if half == 0 else "dve"
                self.CP(eng, s[:, half * 512:(half + 1) * 512], ps[:], [f"ps{pi}"], [key])
            ev = self.DMA("sp", self.out[b, (t - 2) * 128:(t - 1) * 128, :], s[:], self.ost_sems[t % 2], r=[key])
            self.out_evs.append(ev)
        P.phase_end()

    def mod_vectors(self, b, l):
        P = self.P
        P.phase_begin()
        craw = P.psbuf("craw", [24, 128], F32)
        csil = P.psbuf("csil", [128, 24], BF16)
        MT, MP = self.modTs[l], self.modPs[l]
        mtk, mpk = f"modT{l}", f"modP{l}"
        ck = f"craw_p{P.uid}"
        for bb in range(NB):
            self.DMA("sp", craw[bb * 8:bb * 8 + 8, :], self.c[bb].rearrange("(c p) -> c p", p=128), self.misc_sem, w=[ck])
        self.DMA("sp", craw[16:24, :], self.c_ctx.rearrange("(c p) -> c p", p=128), self.misc_sem, w=[ck])
        P.seal(self.misc_sem, [ck])
        self.TR(self.ps[0][:, 0:24], "ps0", craw[:, :], self.identf[0:24, 0:24], [ck, "identf"])
        self.ACT(csil[:], self.ps[0][:, 0:24], AF.Silu, ["ps0"], ["csil"])
        psm = self.ps[1]
        direct = (b == 0 and l == 0)
        if direct:
            fstg = [P.psbuf(f"fstg{i}", [128, 8 * 512], F32) for i in range(2)]
            bstg = [P.psbuf(f"bstg{i}", [128, 8 * 512], BF16) for i in range(2)]
            ceng = ("act", "dve", "pool", "act", "dve", "act", "dve", "pool")
        else:
            ring = self.make_ring("rmod", 2, 8 * 512)
            nxt = self.ring_load(ring, l, "mod", 0, 8 * 512)
        for s in range(12):
            if direct:
                i = s % 2
                fk, bk = f"fstg{i}_p{P.uid}", f"bstg{i}_p{P.uid}"
                src = self.w_mod[l].rearrange("(k p) n -> p k n", p=128)[:, :, s * 512:(s + 1) * 512]
                self.DMA("sp", fstg[i][:].rearrange("p (k n) -> p k n", n=512), src, self.stg_sems[i], w=[fk])
                for k in range(8):
                    self.CP(ceng[k], bstg[i][:, k * 512:(k + 1) * 512], fstg[i][:, k * 512:(k + 1) * 512], [fk], [bk])
                t, key = bstg[i], bk
            else:
                t, key = nxt
                if s + 1 < 12:
                    nxt = self.ring_load(ring, l, "mod", s + 1, 8 * 512)
            wv = t[:].rearrange("p (k n) -> p k n", n=512)
            for q in range(4):
                col = s * 4 + q
                pairs = [(wv[:, k, q * 128:(q + 1) * 128], csil[:].rearrange("p (w k) -> p k w", w=3)[:, k, :]) for k in range(8)]
                self.MM(psm[:, col * 3:col * 3 + 3], "ps1", pairs, [key, "csil"])
        bm = self.V(l, "b_mod", 0, 48)
        self.TT("dve", MT[:], psm[:, 0:144].rearrange("p (c w) -> p c w", w=3), bm.unsqueeze(2).to_broadcast([128, 48, 3]), ALU.add,
                ["ps1", f"vec{l}"], [mtk])
        self.TS("dve", MP[:, 0:8, :], MT[:, 8:16, :], 1.0, None, ALU.add, None, [mtk], [mpk])
        self.TS("dve", MP[:, 8:16, :], MT[:, 32:40, :], 1.0, None, ALU.add, None, [mtk], [mpk])
        self.TT("dve", MP[:, 16:24, :], MT[:, 16:24, :], self.V(l, "b_out", 0, 8).unsqueeze(2).to_broadcast([128, 8, 3]), ALU.mult,
                [mtk, f"vec{l}"], [mpk])
        self.TT("dve", MP[:, 24:32, :], MT[:, 40:48, :], self.V(l, "b2", 0, 8).unsqueeze(2).to_broadcast([128, 8, 3]), ALU.mult,
                [mtk, f"vec{l}"], [mpk])
        if "mod" in self.debug and b == 0 and l == 0:
            o = self.dbg_out("modT", [128, 144])
            self.DMA("sp", o, MT[:].rearrange("p c w -> p (c w)"), self.dbg_sem, r=[mtk])
        P.phase_end()

    def phase_C(self, b, l, last):
        P = self.P
        P.phase_begin()
        oT = self.brT[0]
        knT = P.psbuf("knT", [128, 4, TALL], BF16)
        krT = P.psbuf("krT", [128, TALL], BF16)
        Vaug = P.psbuf("Vaug", [128, NT, 8, 65], BF16)
        uT = P.psbuf("uT", [128, 8, 512], BF16)
        xgf = P.psbuf("xg", [128, 1536], BF16)
        sqf = P.psbuf("sqb", [128, 1536], BF16)
        rstd = P.psbuf("rstd", [128, 512], F32)
        rstm = P.psbuf("rstm", [128, 4], F32)
        t1 = P.psbuf("t1", [96, 512], F32)
        t2 = P.psbuf("t2", [96, 512], F32)
        ropeC = P.psbuf("ropeC", [96, NLAT], F32)
        ropeS = P.psbuf("ropeS", [96, NLAT], F32)
        ring = self.make_ring("rc", 2, 8 * 384)
        wswk = P.psbuf("wswk", [128, 8, 32], BF16)
        wswq = P.psbuf("wswq", [128, 6, 256], BF16)
        uid = P.uid
        K = lambda n: f"{n}_p{uid}"
        for rb in (0,):
            self.DMA("sp", ropeC[rb:rb + 32, :], self.rope_c, self.misc_sem, w=[K("ropeC")])
            self.DMA("sp", ropeS[rb:rb + 32, :], self.rope_s, self.misc_sem, w=[K("ropeS")])
        P.seal(self.misc_sem, [K("ropeC"), K("ropeS")])
        self.MEMSET("pool", krT[:], 0.0, [K("krT")])
        self.MEMSET("pool", Vaug[:, :, :, 64:65], 1.0, [K("Vaug")])
        g_kv = self.V(l, "kvn_g", 0, 2)
        b_ckv = self.V(l, "b_ckv", 0, 2)
        chunks = [(0, 256, 1)] + [(256 + 512 * i, 512, 0) for i in range(4)]
        xg = xgf[:, 0:1024].rearrange("p (k n) -> p k n", n=512)
        sqb = sqf[:, 0:1024].rearrange("p (k n) -> p k n", n=512)
        for (n0, ln, who) in chunks:
            self.make_uT(uT, K("uT"), n0, ln, who, 0)
            wt, wkey = self.ring_load(ring, l, "in_kv", 0, 8 * 320)
            w3 = wt[:, 0:8 * 320].rearrange("p (k n) -> p k n", n=320)
            for i in range(2):
                ps = self.ps[i]
                self.MM(ps[:, 0:ln], f"ps{i}", [(w3[:, k, i * 128:(i + 1) * 128], uT[:, k, 0:ln]) for k in range(8)], [wkey, K("uT")])
                self.TS("dve", xg[:, i, 0:ln], ps[:, 0:ln], b_ckv[:, i:i + 1], g_kv[:, i:i + 1], ALU.add, ALU.mult, [f"ps{i}", f"vec{l}"], [K("xg")])
                self.ACT(sqb[:, i, 0:ln], ps[:, 0:ln], AF.Square, [f"ps{i}", f"vec{l}"], [K("sqb")], bias=b_ckv[:, i:i + 1])
            ps_r, ps_s = self.ps[2], self.ps[3]
            if who == 0:
                srcv = w3[:, :, 256:288].rearrange("p k (a j f) -> p k a j f", a=2, j=2)
                dstv = wswk[:].rearrange("p k (a j f) -> p k a j f", a=2, j=2)
                for j in range(2):
                    self.CP("pool", dstv[:, :, :, j, :], srcv[:, :, :, 1 - j, :], [wkey], [K("wswk")])
            for rb in (0,):
                self.MM(ps_r[rb:rb + 32, 0:ln], "ps2", [(w3[:, k, 256:288], uT[:, k, 0:ln]) for k in range(8)], [wkey, K("uT")])
                if who == 0:
                    self.MM(ps_s[rb:rb + 32, 0:ln], "ps3", [(wswk[:, k, :], uT[:, k, 0:ln]) for k in range(8)], [K("wswk"), K("uT")])
            for rb in (0,):
                bkr = self.V(l, "b_kr")[rb:rb + 32, :]
                if who == 1:
                    self.TS("dve", krT[rb:rb + 32, n0:n0 + ln], ps_r[rb:rb + 32, 0:ln], bkr, None, ALU.add, None, ["ps2", f"vec{l}"], [K("krT")])
                else:
                    bsw = self.V(l, "b_krsw")[rb:rb + 32, :]
                    p0 = n0 - NCTX
                    self.STT(t1[rb:rb + 32, 0:ln], ps_r[rb:rb + 32, 0:ln], bkr, ropeC[rb:rb + 32, p0:p0 + ln], ALU.add, ALU.mult,
                             ["ps2", f"vec{l}", K("ropeC")], [K("t1")])
                    self.STT(t2[rb:rb + 32, 0:ln], ps_s[rb:rb + 32, 0:ln], bsw, ropeS[rb:rb + 32, p0:p0 + ln], ALU.add, ALU.mult,
                             ["ps3", f"vec{l}", K("ropeS")], [K("t2")])
                    self.TT("pool", krT[rb:rb + 32, n0:n0 + ln], t1[rb:rb + 32, 0:ln], t2[rb:rb + 32, 0:ln], ALU.add, [K("t1"), K("t2")], [K("krT")])
            self.MM(self.ps[4][:, 0:ln], "ps4", [(self.onesb[:], sqb[:, i, 0:ln]) for i in range(2)], ["onesb", K("sqb")])
            self.ACT(rstd[:, 0:ln], self.ps[4][:, 0:ln], AF.Sqrt, ["ps4"], [K("rstd")], bias=self.epsc[:, 0:1], scale=1.0 / 256)
            P.op("dve", lambda e, ln=ln: e.reciprocal(out=rstd[:, 0:ln], in_=rstd[:, 0:ln]), r=[K("rstd")], w=[K("rstd")])
            nt = ln // 128
            for tt in range(nt):
                self.MM(self.ps[5][:, tt:tt + 1], "ps5", [(sqb[:, i, tt * 128:(tt + 1) * 128], self.onesb[:, 0:1]) for i in range(2)], [K("sqb"), "onesb"])
            self.ACT(rstm[:, 0:nt], self.ps[5][:, 0:nt], AF.Sqrt, ["ps5"], [K("rstm")], bias=self.epsc[:, 0:1], scale=1.0 / 256)
            P.op("dve", lambda e, nt=nt: e.reciprocal(out=rstm[:, 0:nt], in_=rstm[:, 0:nt]), r=[K("rstm")], w=[K("rstm")])
            wt, wkey = self.ring_load(ring, l, "ukv", 0, 2 * 512)
            wn = wt[:, 0:1024].rearrange("p (k n) -> p k n", n=512)
            for p in range(4):
                pi = p % 2
                ps = self.ps[pi]
                self.MM(ps[:, 0:ln], f"ps{pi}", [(wn[:, k, p * 128:(p + 1) * 128], xg[:, k, 0:ln]) for k in range(2)], [wkey, K("xg")])
                self.TT("dve", knT[:, p, n0:n0 + ln], ps[:, 0:ln], rstd[:, 0:ln], ALU.mult, [f"ps{pi}", K("rstd")], [K("knT")])
            wt, wkey = self.ring_load(ring, l, "ukv", 1, 2 * 512)
            wv = wt[:, 0:1024].rearrange("p (k n) -> p k n", n=512)
            for tt in range(nt):
                pi = 6 + tt % 2
                ps = self.ps[pi]
                tile = (n0 // 128) + tt
                self.MM(ps[:, 0:512], f"ps{pi}", [(xg[:, k, tt * 128:(tt + 1) * 128], wv[:, k, :]) for k in range(2)], [wkey, K("xg")])
                self.ACT(Vaug[:, tile, :, 0:64], ps[:, 0:512].rearrange("p (h e) -> p h e", e=64), AF.Copy, [f"ps{pi}", K("rstm")], [K("Vaug")],
                         scale=rstm[:, tt:tt + 1])
        if "kv" in self.debug and b == 0 and l == 0:
            o = self.dbg_out("knT", [128, 4 * TALL], BF16)
            self.DMA("sp", o, knT[:].rearrange("p a n -> p (a n)"), self.dbg_sem, r=[K("knT")])
            o = self.dbg_out("krT", [32, TALL], BF16)
            self.DMA("sp", o, krT[0:32, :], self.dbg_sem, r=[K("krT")])
            o = self.dbg_out("Vaug", [128, NT * 8 * 65], BF16)
            self.DMA("sp", o, Vaug[:].rearrange("p a h e -> p (a h e)"), self.dbg_sem, r=[K("Vaug")])
        QC = 256
        xg = xgf[:].rearrange("p (k n) -> p k n", n=QC)
        sqb = sqf[:].rearrange("p (k n) -> p k n", n=QC)
        qnT = P.psbuf("qnT", [128, 8, QC], BF16)
        qrT = P.psbuf("qrT", [128, 8, QC], BF16)
        self.MEMSET("pool", qnT[:], 0.0, [K("qnT")])
        self.MEMSET("pool", qrT[:], 0.0, [K("qrT")])
        P.barrier()
        tq = [(t1[:, i * QC:(i + 1) * QC], t2[:, i * QC:(i + 1) * QC]) for i in range(2)]
        Cp = P.psbuf("Cp", [96, QC], F32)
        Sp = P.psbuf("Sp", [96, QC], F32)
        PT = [P.psbuf(f"PT{i}", [128, QC], BF16) for i in range(4)]
        otm = [P.psbuf(f"otm{i}", [128, 512], BF16) for i in range(2)]
        rden = P.psbuf("rden", [128, 2], F32)
        g_q = self.V(l, "qn_g", 0, 6)
        b_cq = self.V(l, "b_cq", 0, 6)
        qchunks = [(256 + QC * i, QC, 0) for i in range(NLAT // QC)]
        if not last:
            qchunks = [(0, 256, 1)] + qchunks
        pt_i = 0
        for (n0, ln, who) in qchunks:
            self.make_uT(uT, K("uT"), n0, ln, who, 0)
            for s in range(2):
                wt, wkey = self.ring_load(ring, l, "in_cq", s, 8 * 384)
                w3 = wt[:, 0:8 * 384].rearrange("p (k n) -> p k n", n=384)
                for q in range(3):
                    i = s * 3 + q
                    pi = i % 4
                    ps = self.ps[pi]
                    self.MM(ps[:, 0:ln], f"ps{pi}", [(w3[:, k, q * 128:(q + 1) * 128], uT[:, k, 0:ln]) for k in range(8)], [wkey, K("uT")])
                    self.TS("dve", xg[:, i, 0:ln], ps[:, 0:ln], b_cq[:, i:i + 1], g_q[:, i:i + 1], ALU.add, ALU.mult, [f"ps{pi}", f"vec{l}"], [K("xg")])
                    self.ACT(sqb[:, i, 0:ln], ps[:, 0:ln], AF.Square, [f"ps{pi}", f"vec{l}"], [K("sqb")], bias=b_cq[:, i:i + 1])
            self.MM(self.ps[4][:, 0:ln], "ps4", [(self.onesb[:], sqb[:, i, 0:ln]) for i in range(6)], ["onesb", K("sqb")])
            self.ACT(rstd[:, 0:ln], self.ps[4][:, 0:ln], AF.Sqrt, ["ps4"], [K("rstd")], bias=self.epsc[:, 0:1], scale=1.0 / 768)
            P.op("dve", lambda e, ln=ln: e.reciprocal(out=rstd[:, 0:ln], in_=rstd[:, 0:ln]), r=[K("rstd")], w=[K("rstd")])
            wt, wkey = self.ring_load(ring, l, "uq", 0, 6 * 512)
            wqn = wt[:, 0:3072].rearrange("p (k n) -> p k n", n=512)
            for p in range(4):
                pi = 5 + p % 2
                ps = self.ps[pi]
                self.MM(ps[:, 0:ln], f"ps{pi}", [(wqn[:, k, p * 128:(p + 1) * 128], xg[:, k, 0:ln]) for k in range(6)], [wkey, K("xg")])
                for hh in range(2):
                    self.TT("dve", qnT[hh * 64:hh * 64 + 64, 2 * p + hh, 0:ln], ps[hh * 64:hh * 64 + 64, 0:ln], rstd[hh * 64:hh * 64 + 64, 0:ln],
                            ALU.mult, [f"ps{pi}", K("rstd")], [K("qnT")])
            if who == 0:
                p0 = n0 - NCTX
                for rb in (0,):
                    self.TT("pool", Cp[rb:rb + 32, 0:ln], ropeC[rb:rb + 32, p0:p0 + ln], rstd[rb:rb + 32, 0:ln], ALU.mult, [K("ropeC"), K("rstd")], [K("Cp")])
                    self.TT("pool", Sp[rb:rb + 32, 0:ln], ropeS[rb:rb + 32, p0:p0 + ln], rstd[rb:rb + 32, 0:ln], ALU.mult, [K("ropeS"), K("rstd")], [K("Sp")])
            wt, wkey = self.ring_load(ring, l, "uq", 1, 6 * 512)
            wqr = wt[:, 0:3072].rearrange("p (k n) -> p k n", n=512)
            if who == 0:
                for k in range(6):
                    srcv = wqr[:, k, 0:256].rearrange("p (h a j f) -> p h a j f", h=8, a=2, j=2)
                    dstv = wswq[:, k, :].rearrange("p (h a j f) -> p h a j f", h=8, a=2, j=2)
                    for j in range(2):
                        self.CP("pool", dstv[:, :, :, j, :], srcv[:, :, :, 1 - j, :], [wkey], [K("wswq")])
            for h in range(8):
                rb = 0
                hp = h
                pr, psw = self.ps[(2 * h) % 4], self.ps[(2 * h + 1) % 4]
                kr_, ks_ = f"ps{(2 * h) % 4}", f"ps{(2 * h + 1) % 4}"
                self.MM(pr[rb:rb + 32, 0:ln], kr_, [(wqr[:, k, h * 32:(h + 1) * 32], xg[:, k, 0:ln]) for k in range(6)], [wkey, K("xg")])
                if who == 1:
                    self.TT("dve", qrT[rb:rb + 32, hp, 0:ln], pr[rb:rb + 32, 0:ln], rstd[rb:rb + 32, 0:ln], ALU.mult, [kr_, K("rstd")], [K("qrT")])
                else:
                    self.MM(psw[rb:rb + 32, 0:ln], ks_, [(wswq[:, k, h * 32:(h + 1) * 32], xg[:, k, 0:ln]) for k in range(6)], [K("wswq"), K("xg")])
                    ta, tb = tq[h % 2]
                    self.TT("dve", ta[0:32, 0:ln], pr[0:32, 0:ln], Cp[0:32, 0:ln], ALU.mult, [kr_, K("Cp")], [K(f"tqa{h % 2}")])
                    self.TT("dve", tb[0:32, 0:ln], psw[0:32, 0:ln], Sp[0:32, 0:ln], ALU.mult, [ks_, K("Sp")], [K(f"tqb{h % 2}")])
                    self.TT("pool", qrT[0:32, hp, 0:ln], ta[0:32, 0:ln], tb[0:32, 0:ln], ALU.add, [K(f"tqa{h % 2}"), K(f"tqb{h % 2}")], [K("qrT")])
            ktiles = list(range(2)) if who == 1 else list(range(NT))
            nk = len(ktiles)
            nq = ln // 128
            items = [(h, ki) for h in range(8) for ki in range(nk)]

            def issue_scores(j):
                h, ki = items[j]
                kt_ = ktiles[ki]
                hp, hb = h // 2, (h % 2) * 64
                si = j % 4
                pairs = [(knT[:, hp, kt_ * 128:(kt_ + 1) * 128], qnT[:, h, 0:ln]),
                         (krT[:, kt_ * 128:(kt_ + 1) * 128], qrT[:, h, 0:ln])]
                self.MM(self.ps[si][:, 0:ln], f"ps{si}", pairs, [K("knT"), K("qnT"), K("krT"), K("qrT")])

            LOOK = 3
            for j in range(min(LOOK, len(items))):
                issue_scores(j)
            for j, (h, ki) in enumerate(items):
                kt_ = ktiles[ki]
                si = j % 4
                ob = [self.ps[4 + (h % 2) * 2 + qt] for qt in range(nq)]
                obk = [f"ps{4 + (h % 2) * 2 + qt}" for qt in range(nq)]
                pt = PT[j % 4]
                ptk = K(f"PT{j % 4}")
                self.ACT(pt[:, 0:ln], self.ps[si][:, 0:ln], AF.Exp, [f"ps{si}"], [ptk], scale=MLA_SCALE)
                if j + LOOK < len(items):
                    issue_scores(j + LOOK)
                for qt in range(nq):
                    self.MM(ob[qt][:, 0:65], obk[qt], [(pt[:, qt * 128:(qt + 1) * 128], Vaug[:, kt_, h, :])], [ptk, K("Vaug")],
                            start=(ki == 0), stop=(ki == nk - 1))
                if ki == nk - 1:
                    for qt in range(nq):
                        P.op("dve", lambda e, qt=qt, o=ob[qt]: e.reciprocal(out=rden[:, qt:qt + 1], in_=o[:, 64:65]), r=[obk[qt]], w=[K("rden")])
                        self.TS("dve", otm[qt][:, h * 64:(h + 1) * 64], ob[qt][:, 0:64], rden[:, qt:qt + 1], None, ALU.mult, None,
                                [obk[qt], K("rden")], [K(f"otm{qt}")])
            for qt in range(nq):
                psb = self.ps[3][:].bitcast(BF16)
                for kk in range(4):
                    self.TR(psb[:, kk * 128:(kk + 1) * 128], "ps3", otm[qt][:, kk * 128:(kk + 1) * 128], self.identb[:], [K(f"otm{qt}"), "identb"])
                tok = n0 + qt * 128
                self.CP("act", oT[:, :, tok:tok + 128], psb[:, 0:512].rearrange("p (k n) -> p k n", n=128), ["ps3"], [f"oT{tok // 128}"])
        if "oT" in self.debug and b == 0 and l == 0:
            o = self.dbg_out("oT", [128, 4 * TALL], BF16)
            self.DMA("sp", o, oT[:].rearrange("p a n -> p (a n)"), self.dbg_sem, r=tkeys("oT", 0, TALL))
        P.phase_end()

    def phase_A(self, b, l, last):
        P = self.P
        P.phase_begin()
        hcT = self.brT[1]
        uid = P.uid
        K = lambda n: f"{n}_p{uid}"
        WC, WL = 15 + NCTX + 15, 15 + NLAT + 15
        Hb = P.psbuf("Hb", [128, 4, WC + WL], BF16)
        uT2 = [P.psbuf(f"uT{i}", [128, 8, 512], BF16) for i in range(2)]
        sg = P.psbuf("sg", [128, 512], F32)
        acc = P.psbuf("acc", [128, 4, 512], F32)
        dg = [P.psbuf(f"dg{i}", [128, 128], BF16) for i in range(6)]
        tmp = dict(sq=P.psbuf("lsq", [128, 512], F32), mean=P.psbuf("lmean", [128, 512], F32),
                   rstd=P.psbuf("lrstd", [128, 512], F32), t1=P.psbuf("lt1", [128, 512], F32))
        ring = self.make_ring("ra", 2, 8 * 512)
        for blk in range(4):
            for (o, n) in ((0, 15), (15 + NCTX, 15), (WC, 15), (WC + 15 + NLAT, 15)):
                self.MEMSET("pool", Hb[:, blk, o:o + n], 0.0, [K("Hb")])
        chunks = ([(0, 256, 1)] if not last else []) + [(256 + 512 * i, 512, 0) for i in range(4)]

        def hoff(n0):
            return 15 + n0 if n0 < NCTX else WC + 15 + (n0 - NCTX)

        b_a = self.V(l, "b_a", 0, 8)
        for ci_, (n0, ln, who) in enumerate(chunks):
            uT, uk = uT2[ci_ % 2], K(f"uT{ci_ % 2}")
            self.make_uT(uT, uk, n0, ln, who, 0)
            for s in range(2):
                wt, wkey = self.ring_load(ring, l, "in_a", s, 8 * 512)
                w3 = wt[:].rearrange("p (k n) -> p k n", n=512)
                for q in range(2):
                    blk = s * 2 + q
                    pv, pg = self.ps[(blk * 2) % 6], self.ps[(blk * 2 + 1) % 6]
                    kv, kg = f"ps{(blk * 2) % 6}", f"ps{(blk * 2 + 1) % 6}"
                    self.MM(pv[:, 0:ln], kv, [(w3[:, k, q * 128:(q + 1) * 128], uT[:, k, 0:ln]) for k in range(8)], [wkey, uk])
                    self.MM(pg[:, 0:ln], kg, [(w3[:, k, 256 + q * 128:256 + (q + 1) * 128], uT[:, k, 0:ln]) for k in range(8)], [wkey, uk])
                    self.ACT(sg[:, 0:ln], pg[:, 0:ln], AF.Sigmoid, [kg, f"vec{l}"], [K("sg")], bias=b_a[:, 4 + blk:5 + blk])
                    c0 = hoff(n0)
                    self.STT(Hb[:, blk, c0:c0 + ln], pv[:, 0:ln], b_a[:, blk:blk + 1], sg[:, 0:ln], ALU.add, ALU.mult,
                             [kv, K("sg"), f"vec{l}"], [K("Hb")])
        wdw = self.V(l, "w_dw", 0, 124)
        bdw = self.V(l, "b_dw", 0, 4)
        gcols = [self.V(l, "cn_g", i) for i in range(4)]
        bcols = [self.V(l, "cn_b", i) for i in range(4)]
        di = 0
        for (n0, ln, who) in chunks:
            c0 = hoff(n0) - 15
            for blk in range(4):
                ps = self.ps[blk]
                for k in range(31):
                    d_ = dg[di % 6]
                    dk = K(f"dg{di % 6}")
                    di += 1
                    wcol = wdw[:, k * 4 + blk:k * 4 + blk + 1]
                    if di % 2 == 0:
                        self.TS("dve", d_[:], self.identf[:], wcol, None, ALU.mult, None, ["identf", f"vec{l}"], [dk])
                    else:
                        self.ACT(d_[:], self.identf[:], AF.Copy, ["identf", f"vec{l}"], [dk], scale=wcol)
                    self.MM(ps[:, 0:ln], f"ps{blk}", [(d_[:], Hb[:, blk, c0 + k:c0 + k + ln])], [dk, K("Hb")], start=(k == 0), stop=(k == 30))
                self.ACT(acc[:, blk, 0:ln], ps[:, 0:ln], AF.Identity, [f"ps{blk}", f"vec{l}"], [K(f"acc{blk}")], bias=bdw[:, blk:blk + 1])
            self.fm_layernorm([acc[:, i, 0:ln] for i in range(4)], [[K(f"acc{i}")] for i in range(4)],
                              [hcT[:, i, n0:n0 + ln] for i in range(4)], [tkeys("hcT", n0, ln)] * 4, ln, gcols, bcols, 512.0, tmp, func=AF.Silu)
        if "hcT" in self.debug and b == 0 and l == 0:
            o = self.dbg_out("hcT", [128, 4 * TALL], BF16)
            self.DMA("sp", o, hcT[:].rearrange("p a n -> p (a n)"), self.dbg_sem, r=tkeys("hcT", 0, TALL))
        P.phase_end()

    def phase_B(self, b, l, last):
        P = self.P
        P.phase_begin()
        hmT = self.brT[2]
        uid = P.uid
        K = lambda n: f"{n}_p{uid}"
        uT = P.psbuf("uT", [128, 8, 512], BF16)
        ring = self.make_ring("rb", 2, 8 * 512)
        wg = P.psbuf("wg", [128, 8 * 16], BF16)
        self.ensure_cast(l, "in_gif")
        self.DMA("sp", wg[:], self.S[l]["in_gif"][0], self.misc_sem, r=[f"S{l}in_gif"], w=[K("wg")])
        P.seal(self.misc_sem, [K("wg")])
        wg3 = wg[:].rearrange("p (k n) -> p k n", n=16)
        pos_f = list(range(NT))
        order_b = [1, 0] + list(range(NT - 1, 1, -1))
        pos_b = [0] * NT
        for i, c in enumerate(order_b):
            pos_b[c] = i
        chunks = [(0, 256, 1)] + [(256 + 512 * i, 512, 0) for i in range(4)]
        qT = P.psbuf("qT", [128, TALL], BF16)
        kT = P.psbuf("kT", [128, TALL], BF16)
        vT = P.psbuf("vT", [128, 512], BF16)
        sgo = P.psbuf("sgo", [128, TALL], BF16)
        Va = P.psbuf("Va", [128, NT, 129], BF16)
        Hd = [P.psbuf(f"Hd{d}", [128, NT, 129], F32) for d in range(2)]
        rin = P.psbuf("rin", [128, 2, NT], F32)
        hflat = Hd[1][:].rearrange("p a b -> p (a b)")
        self.MEMSET("pool", Va[:, :, 128:129], 1.0, [K("Va")])
        GT = hflat[:, 1152:1152 + 288].rearrange("p (d k n) -> p d k n", d=2, k=2)
        G2 = [[hflat[0:72, 640 + (d * 2 + k) * 128:640 + (d * 2 + k + 1) * 128] for k in range(2)] for d in range(2)]
        gbias = P.psbuf("gbias", [72, 4], F32)
        lf, cum, dd, ee, th = [hflat[0:72, i * 128:(i + 1) * 128] for i in range(5)]
        col = P.psbuf("col", [72, 4], F32)
        rowM = P.psbuf("rowM", [1, 72], F32)
        rowB = P.psbuf("rowB", [1, 72], F32)
        rowS = P.psbuf("rowS", [1, 76], F32)
        rowMM = P.psbuf("rowMM", [1, 72], F32)
        rowA = P.psbuf("rowA", [1, 72], F32)
        mmc = P.psbuf("mmc", [72, 1], F32)
        ETAB = P.psbuf("ETAB", [128, 2, 72], F32)
        TTAB = P.psbuf("TTAB", [128, 2, 72], F32)
        ATAB = P.psbuf("ATAB", [128, 2, 72], F32)
        bi = self.b_in[l]
        for d in range(2):
            for k in range(2):
                o = O_GIF + (d * 2 + k) * 4
                for pp in range(NT):
                    pass
        Cst = [P.psbuf(f"Cst{d}", [128, 129], F32) for d in range(2)]
        Cop = [P.psbuf(f"Cop{d}", [128, 129], BF16) for d in range(2)]
        Ke2 = [[P.psbuf(f"Ke{d}{i}", [128, 128], BF16) for i in range(2)] for d in range(2)]
        Pm2 = [[P.psbuf(f"Pm{d}{i}", [128, 128], BF16) for i in range(2)] for d in range(2)]
        dn = P.psbuf("dn", [128, 8], F32)
        st6 = P.psbuf("st6", [128, 6], F32)
        mv = P.psbuf("mv", [128, 4], F32)
        st6a = P.psbuf("st6a", [128, NT, 6], F32)
        mva = P.psbuf("mva", [128, NT, 2], F32)
        rsa = P.psbuf("rsa", [128, NT], F32)
        b_qkv = self.V(l, "b_qkv", 0, 12)
        b_o = self.V(l, "b_o", 0, 4)
        mlg = self.V(l, "ml_g", 0, 4)
        gb = self.gifb[l]
        for h in range(4):
            for (n0, ln, who) in chunks:
                self.make_uT(uT, K("uT"), n0, ln, who, 0)
                wt, wkey = self.ring_load(ring, l, "in_head", h, 8 * 512)
                w3 = wt[:].rearrange("p (k n) -> p k n", n=512)
                nt = ln // 128
                outs = []
                for i in range(4):
                    ps = self.ps[i]
                    self.MM(ps[:, 0:ln], f"ps{i}", [(w3[:, k, i * 128:(i + 1) * 128], uT[:, k, 0:ln]) for k in range(8)], [wkey, K("uT")])
                bq = b_qkv[:, h:h + 1]
                bk = b_qkv[:, 4 + h:5 + h]
                bv = b_qkv[:, 8 + h:9 + h]
                self.TS("dve", qT[:, n0:n0 + ln], self.ps[0][:, 0:ln], bq, None, ALU.add, None, ["ps0", f"vec{l}"], [K("qT")])
                self.TS("dve", kT[:, n0:n0 + ln], self.ps[1][:, 0:ln], bk, KSCALE, ALU.add, ALU.mult, ["ps1", f"vec{l}"], [K("kT")])
                self.ACT(vT[:, 0:ln], self.ps[2][:, 0:ln], AF.Identity, ["ps2", f"vec{l}"], [K("vT")], bias=bv)
                self.ACT(sgo[:, n0:n0 + ln], self.ps[3][:, 0:ln], AF.Sigmoid, ["ps3", f"vec{l}"], [K("sgo")], bias=b_o[:, h:h + 1])
                psb = self.ps[4][:].bitcast(BF16)
                for tt in range(nt):
                    self.TR(psb[:, tt * 128:(tt + 1) * 128], "ps4", vT[:, tt * 128:(tt + 1) * 128], self.identb[:], [K("vT"), "identb"])
                t0 = n0 // 128
                self.CP("act", Va[:, t0:t0 + nt, 0:128], psb[:, 0:nt * 128].rearrange("p (t n) -> p t n", n=128), ["ps4"], [K("Va")])
                if h == 0:
                    for tt in range(nt):
                        c = t0 + tt
                        for d, pos in ((0, pos_f[c]), (1, pos_b[c])):
                            pg = self.ps[5]
                            self.MM(pg[:, d * 256 + pos * 8: d * 256 + pos * 8 + 8], "ps5",
                                    [(uT[:, k, tt * 128:(tt + 1) * 128], wg3[:, k, d * 8:d * 8 + 8]) for k in range(8)], [K("uT"), K("wg")])
            if h == 0:
                pg = self.ps[5]
                for d in range(2):
                    src = pg[:, d * 256:d * 256 + NT * 8].rearrange("p (c k h) -> p k c h", k=2, h=4)
                    bsrc = gb[:, d * 8:d * 8 + 8].rearrange("p (k h) -> p k h", h=4).unsqueeze(2).to_broadcast([128, 2, NT, 4])
                    self.TT("dve", GT[:, d, :, :].rearrange("p k (c h) -> p k c h", h=4), src, bsrc, ALU.add, ["ps5", f"gifb{l}"], [K("GT")])
                for d in range(2):
                    for k in range(2):
                        pi = 6 + k
                        self.TR(self.ps[pi][0:72, 0:128], f"ps{pi}", GT[:, d, k, :], self.identf[:], [K("GT"), "identf"])
                        self.CP("dve", G2[d][k][:], self.ps[pi][0:72, 0:128], [f"ps{pi}"], [K(f"G2_{d}{k}")])
                    gi, gf = G2[d][0], G2[d][1]
                    kgi, kgf = K(f"G2_{d}0"), K(f"G2_{d}1")
                    self.ACT(lf[:], gf[:], AF.Exp, [kgf], [K("lf")], scale=-1.0)
                    self.ACT(lf[:], lf[:], AF.Ln, [K("lf")], [K("lf")], bias=self.onec[0:72, 0:1])
                    self.TS("dve", lf[:], lf[:], -1.0, None, ALU.mult, None, [K("lf")], [K("lf")])
                    P.op("dve", lambda e: e.tensor_tensor_scan(out=cum[:], data0=self.onesf[0:72, :], data1=lf[:], initial=0.0, op0=ALU.mult, op1=ALU.add),
                         r=[K("lf"), "onesf"], w=[K("cum")])
                    self.CP("dve", col[:, 1:2], cum[:, 127:128], [K("cum")], [K("col")])
                    if d == 1:
                        self.TT("dve", cum[:], lf[:], cum[:], ALU.subtract, [K("lf"), K("cum")], [K("cum")])
                        self.TS("dve", cum[:], cum[:], col[:, 1:2], None, ALU.add, None, [K("cum"), K("col")], [K("cum")])
                    self.TT("dve", dd[:], gi[:], cum[:], ALU.subtract, [kgi, K("cum")], [K("dd")])
                    P.op("dve", lambda e: e.tensor_reduce(out=col[:, 0:1], in_=dd[:], axis=AX.X, op=ALU.max), r=[K("dd")], w=[K("col")])
                    self.TR(self.ps[6][0:2, 0:72], "ps6", col[:, 0:2], self.identf[0:72, 0:72], [K("col"), "identf"])
                    self.CP("dve", rowM[:], self.ps[6][0:1, 0:72], ["ps6"], [K("rowM")])
                    self.MM(self.ps[7][0:1, 0:72], "ps7", [(col[:, 1:2], self.identf[0:72, 0:72])], [K("col"), "identf"])
                    self.CP("dve", rowB[:], self.ps[7][0:1, 0:72], ["ps7"], [K("rowB")])
                    self.MEMSET("dve", rowS[:, 0:4], 0.0, [K("rowS")])
                    for hh in range(4):
                        Mv = rowM[:].rearrange("p (c h) -> p c h", h=4)[:, :, hh]
                        Bv = rowB[:].rearrange("p (c h) -> p c h", h=4)[:, :, hh]
                        Sv = rowS[:, 4:76].rearrange("p (c h) -> p c h", h=4)[:, :, hh]
                        P.op("dve", lambda e, Mv=Mv, Bv=Bv, Sv=Sv: e.tensor_tensor_scan(out=Sv, data0=Mv, data1=Bv, initial=0.0, op0=ALU.max, op1=ALU.add),
                             r=[K("rowM"), K("rowB")], w=[K("rowS")])
                    self.TT("dve", rowMM[:], rowM[:], rowS[:, 0:72], ALU.max, [K("rowM"), K("rowS")], [K("rowMM")])
                    self.TT("dve", rowA[:], rowS[:, 0:72], rowMM[:], ALU.subtract, [K("rowS"), K("rowMM")], [K("rowA")])
                    self.ACT(rowA[:], rowA[:], AF.Exp, [K("rowA")], [K("rowA")])
                    self.MM(self.ps[6][0:72, 0:1], "ps6", [(rowMM[:], self.onesf[0:1, 0:1])], [K("rowMM"), "onesf"])
                    self.CP("dve", mmc[:], self.ps[6][0:72, 0:1], ["ps6"], [K("mmc")])
                    self.MM(self.ps[7][:, 0:72], "ps7", [(self.onesf[0:1, :], rowA[:])], [K("rowA"), "onesf"])
                    self.CP("dve", ATAB[:, d, :], self.ps[7][:, 0:72], ["ps7"], [K("ATAB")])
                    self.TS("dve", ee[:], dd[:], mmc[:, 0:1], None, ALU.subtract, None, [K("dd"), K("mmc")], [K("ee")])
                    self.ACT(ee[:], ee[:], AF.Exp, [K("ee")], [K("ee")])
                    self.TS("dve", th[:], cum[:], mmc[:, 0:1], None, ALU.add, None, [K("cum"), K("mmc")], [K("th")])
                    self.ACT(th[:], th[:], AF.Exp, [K("th")], [K("th")], scale=-1.0)
                    self.TR(self.ps[6][:, 0:72], "ps6", ee[:], self.identf[0:72, 0:72], [K("ee"), "identf"])
                    self.CP("dve", ETAB[:, d, :], self.ps[6][:, 0:72], ["ps6"], [K("ETAB")])
                    self.TR(self.ps[7][:, 0:72], "ps7", th[:], self.identf[0:72, 0:72], [K("th"), "identf"])
                    self.CP("dve", TTAB[:, d, :], self.ps[7][:, 0:72], ["ps7"], [K("TTAB")])
                if "gates" in self.debug and b == 0 and l == 0:
                    for nm, t in (("ETAB", ETAB), ("TTAB", TTAB), ("ATAB", ATAB)):
                        o = self.dbg_out(nm, [128, 144])
                        self.DMA("sp", o, t[:].rearrange("p d n -> p (d n)"), self.dbg_sem, r=[K(nm)])
            if h == 0:
                P.barrier()
            for d in range(2):
                self.MEMSET("pool", Cst[d][:], 0.0, [K(f"Cst{d}")])
            done = [0] * NT
            orders = [list(range(NT)), order_b]
            masks = [self.maskf, self.maskb]
            mkeys = ["maskf", "maskb"]

            def banks(d, s_):
                pb = d * 4
                return pb, pb + 1, pb + 2 + (s_ % 2)

            def part1(d, s_):
                c = orders[d][s_]
                r = s_ * 4 + h
                e_col = ETAB[:, d, r:r + 1]
                tok = slice(c * 128, (c + 1) * 128)
                tb, sb, nb = banks(d, s_)
                ke, pm = Ke2[d][s_ % 2], Pm2[d][s_ % 2]
                kek, pmk = K(f"Ke{d}{s_ % 2}"), K(f"Pm{d}{s_ % 2}")
                psb = self.ps[tb][:].bitcast(BF16)
                self.TR(psb[:, 0:128], f"ps{tb}", kT[:, tok], self.identb[:], [K("kT"), "identb"])
                self.ACT(ke[:], psb[:, 0:128], AF.Copy, [f"ps{tb}", K("ETAB")], [kek], scale=e_col)
                self.MM(self.ps[sb][:, 0:128], f"ps{sb}", [(kT[:, tok], qT[:, tok])], [K("kT"), K("qT")])
                self.STT(pm[:], self.ps[sb][:, 0:128], e_col, masks[d][:], ALU.mult, ALU.mult, [f"ps{sb}", K("ETAB"), mkeys[d]], [pmk])

            def part1b(d, s_):
                c = orders[d][s_]
                tb, sb, nb = banks(d, s_)
                pm = Pm2[d][s_ % 2]
                pmk = K(f"Pm{d}{s_ % 2}")
                self.MM(self.ps[nb][:, 0:129], f"ps{nb}", [(pm[:], Va[:, c, :])], [pmk, K("Va")], start=True, stop=False)

            def cop(d, s_):
                r = s_ * 4 + h
                a_col = ATAB[:, d, r:r + 1]
                self.TS("pool", Cop[d][:], Cst[d][:], a_col, None, ALU.mult, None, [K(f"Cst{d}"), K("ATAB")], [K(f"Cop{d}")])

            def part2(d, s_):
                c = orders[d][s_]
                r = s_ * 4 + h
                t_col = TTAB[:, d, r:r + 1]
                a_col = ATAB[:, d, r:r + 1]
                tok = slice(c * 128, (c + 1) * 128)
                tb, sb, nb = banks(d, s_)
                ke = Ke2[d][s_ % 2]
                kek = K(f"Ke{d}{s_ % 2}")
                pn = self.ps[nb]
                self.MM(pn[:, 0:129], f"ps{nb}", [(qT[:, tok], Cop[d][:])], [K("qT"), K(f"Cop{d}")], start=False, stop=True)
                pst = self.ps[tb]
                self.MM(pst[:, 0:129], f"ps{tb}", [(ke[:], Va[:, c, :])], [kek, K("Va")])
                self.STT(Cst[d][:], Cst[d][:], a_col, pst[:, 0:129], ALU.mult, ALU.add, [K(f"Cst{d}"), K("ATAB"), f"ps{tb}"], [K(f"Cst{d}")])
                self.CP("act", Hd[d][:, s_, :], pn[:, 0:129], [f"ps{nb}"], [K(f"Hd{d}_{s_}")])

            for d in range(2):
                part1(d, 0)
            for d in range(2):
                part1b(d, 0)
            for step in range(NT):
                for d in range(2):
                    cop(d, step)
                if step + 1 < NT:
                    for d in range(2):
                        part1(d, step + 1)
                for d in range(2):
                    part2(d, step)
                if step + 1 < NT:
                    for d in range(2):
                        part1b(d, step + 1)
            for d in range(2):
                dk_all = [K(f"Hd{d}_{i}") for i in range(NT)]
                dnv = Hd[d][:, :, 128]
                thv = TTAB[:, d, :].rearrange("p (c h) -> p c h", h=4)[:, :, h]
                self.TT("dve", rin[:, d, :], dnv, thv, ALU.max, dk_all + [K("TTAB")], [K("rin")])
                self.STT(rin[:, d, :], dnv, -1.0, rin[:, d, :], ALU.mult, ALU.max, dk_all + [K("rin")], [K("rin")])
                P.op("dve", lambda e, d=d: e.reciprocal(out=rin[:, d, :], in_=rin[:, d, :]), r=[K("rin")], w=[K("rin")])
            Hs = Hd[0]
            for c in range(NT):
                pf_, pb_ = pos_f[c], pos_b[c]
                self.TS("pool", Hd[1][:, pb_, 0:128], Hd[1][:, pb_, 0:128], rin[:, 1, pb_:pb_ + 1], None, ALU.mult, None,
                        [K(f"Hd1_{pb_}"), K("rin")], [K(f"Hd1_{pb_}")])
                self.STT(Hs[:, c, 0:128], Hs[:, c, 0:128], rin[:, 0, pf_:pf_ + 1], Hd[1][:, pb_, 0:128], ALU.mult, ALU.add,
                         [K(f"Hd0_{c}"), K(f"Hd1_{pb_}"), K("rin")], [K(f"Hd0_{c}")])
            for c in range(NT):
                P.op("dve", lambda e, c=c: e.bn_stats(out=st6a[:, c, :], in_=Hs[:, c, 0:128]), r=[K(f"Hd0_{c}")], w=[K("st6a")])
                P.op("dve", lambda e, c=c: e.bn_aggr(out=mva[:, c, :], in_=st6a[:, c, :]), r=[K("st6a")], w=[K("mva")])
            self.ACT(rsa[:], mva[:, :, 1], AF.Sqrt, [K("mva")], [K("rsa")], bias=self.epsc[:, 0:1])
            P.op("dve", lambda e: e.reciprocal(out=rsa[:], in_=rsa[:]), r=[K("rsa")], w=[K("rsa")])
            groups = [(0, 2), (2, 4), (6, 4), (10, 4), (14, 4)]
            for gi, (c0, ncg) in enumerate(groups):
                hk = [K(f"Hd0_{c}") for c in range(c0, c0 + ncg)]
                hv = Hs[:, c0:c0 + ncg, 0:128]
                self.TT("dve", hv, hv, mva[:, c0:c0 + ncg, 0:1].to_broadcast([128, ncg, 128]), ALU.subtract, hk + [K("mva")], hk)
                self.TT("pool", hv, hv, rsa[:, c0:c0 + ncg].unsqueeze(2).to_broadcast([128, ncg, 128]), ALU.mult, hk + [K("rsa")], hk)
                pf = self.ps[gi % 4]
                for i in range(ncg):
                    self.TR(pf[:, i * 128:(i + 1) * 128], f"ps{gi % 4}", Hs[:, c0 + i, 0:128], self.identf[:], hk + ["identf"])
                tsl = slice(c0 * 128, (c0 + ncg) * 128)
                self.STT(hmT[:, h, tsl], pf[:, 0:ncg * 128], mlg[:, h:h + 1], sgo[:, tsl], ALU.mult, ALU.mult,
                         [f"ps{gi % 4}", f"vec{l}", K("sgo")], [f"hmT{c}" for c in range(c0, c0 + ncg)])
        if "hmT" in self.debug and b == 0 and l == 0:
            o = self.dbg_out("hmT", [128, 4 * TALL], BF16)
            self.DMA("sp", o, hmT[:].rearrange("p a n -> p (a n)"), self.dbg_sem, r=tkeys("hmT", 0, TALL))
        P.phase_end()

    def phase_D1(self, b, l, last):
        P = self.P
        P.phase_begin()
        uid = P.uid
        K = lambda n: f"{n}_p{uid}"
        uT = P.psbuf("uT", [128, 8, 512], BF16)
        accm = P.psbuf("accm", [128, 8, 512], F32)
        mT = P.psbuf("mT", [128, 8, 512], BF16)
        sg2 = [P.psbuf(f"sg{i}", [128, 512], F32) for i in range(2)]
        tt2 = [P.psbuf(f"tt{i}", [128, 512], F32) for i in range(2)]
        cnt = [0]
        tmp = dict(sq=P.psbuf("lsq", [128, 512], F32), mean=P.psbuf("lmean", [128, 512], F32),
                   rstd=P.psbuf("lrstd", [128, 512], F32), t1=P.psbuf("lt1", [128, 512], F32))
        ring = self.make_ring("rd", 2, 12 * 512)
        srcs = [self.brT[1], self.brT[2], self.brT[0]]
        skeys = ["hcT", "hmT", "oT"]
        b_g = self.V(l, "b_g", 0, 24)
        chunks = ([(0, 256, 1)] if not last else []) + [(256 + 512 * i, 512, 0) for i in range(4)]
        def do_ln1(c):
            if c is None:
                return
            m0, mln = c
            zs = [self.xT[:, fc, m0:m0 + mln] for fc in range(8)]
            xk = tkeys("xT", m0, mln)
            self.fm_layernorm(zs, [xk] * 8, zs, [xk] * 8, mln, [self.V(l, "ln1_g", i) for i in range(8)],
                              [self.V(l, "ln1_b", i) for i in range(8)], 1024.0, tmp)

        pending_ln = None
        for (n0, ln, who) in chunks:
            self.make_uT(uT, K("uT"), n0, ln, who, 0)
            for br in range(3):
                for g in range(2):
                    wt, wkey = self.ring_load(ring, l, "merge", br * 2 + g, 12 * 512)
                    w3 = wt[:].rearrange("p (k n) -> p k n", n=512)
                    for q in range(4):
                        fc = g * 4 + q
                        py, pg = self.ps[(2 * q) % 6], self.ps[(2 * q + 1) % 6]
                        ky, kg = f"ps{(2 * q) % 6}", f"ps{(2 * q + 1) % 6}"
                        self.MM(py[:, 0:ln], ky, [(w3[:, k, q * 128:(q + 1) * 128], srcs[br][:, k, n0:n0 + ln]) for k in range(4)],
                                [wkey] + tkeys(skeys[br], n0, ln))
                        self.MM(pg[:, 0:ln], kg, [(w3[:, 4 + k, q * 128:(q + 1) * 128], uT[:, k, 0:ln]) for k in range(8)], [wkey, K("uT")])
                        bi_ = cnt[0] % 2
                        cnt[0] += 1
                        sg, tt_ = sg2[bi_], tt2[bi_]
                        sgk, ttk = K(f"sg{bi_}"), K(f"tt{bi_}")
                        self.ACT(sg[:, 0:ln], pg[:, 0:ln], AF.Sigmoid, [kg, f"vec{l}"], [sgk], bias=b_g[:, br * 8 + fc:br * 8 + fc + 1])
                        if br == 0:
                            self.TT("dve", accm[:, fc, 0:ln], py[:, 0:ln], sg[:, 0:ln], ALU.mult, [ky, sgk], [K(f"accm{fc}")])
                        else:
                            self.TT("dve", tt_[:, 0:ln], py[:, 0:ln], sg[:, 0:ln], ALU.mult, [ky, sgk], [ttk])
                            if br == 1:
                                self.TT("pool", accm[:, fc, 0:ln], accm[:, fc, 0:ln], tt_[:, 0:ln], ALU.add, [ttk, K(f"accm{fc}")], [K(f"accm{fc}")])
                            else:
                                self.TT("pool", mT[:, fc, 0:ln], accm[:, fc, 0:ln], tt_[:, 0:ln], ALU.add, [ttk, K(f"accm{fc}")], [K("mT")])
            do_ln1(pending_ln)
            pending_ln = None
            for s in range(2):
                wt, wkey = self.ring_load(ring, l, "wout", s, 8 * 512)
                w3 = wt[:, 0:8 * 512].rearrange("p (k n) -> p k n", n=512)
                for q in range(4):
                    fc = s * 4 + q
                    pi = fc % 6
                    ps = self.ps[pi]
                    self.MM(ps[:, 0:ln], f"ps{pi}", [(w3[:, k, q * 128:(q + 1) * 128], mT[:, k, 0:ln]) for k in range(8)], [wkey, K("mT")])
                    wc = self.cur_b if who == 0 else 2
                    g1 = self.modT[:, 16 + fc, wc:wc + 1]
                    bg1 = self.modP[:, 16 + fc, wc:wc + 1]
                    bi_ = cnt[0] % 2
                    cnt[0] += 1
                    tt_, ttk = tt2[bi_], K(f"tt{bi_}")
                    self.TS("dve", tt_[:, 0:ln], ps[:, 0:ln], g1, bg1, ALU.mult, ALU.add, [f"ps{pi}"] + list(self.mk), [ttk])
                    xs = self.xT[:, fc, n0:n0 + ln]
                    self.STT(xs, xs, ALPHA, tt_[:, 0:ln], ALU.mult, ALU.add, tkeys("xT", n0, ln) + [ttk], tkeys("xT", n0, ln))
            pending_ln = (n0, ln)
        do_ln1(pending_ln)
        if "x1" in self.debug and b == 0 and l == 0:
            o = self.dbg_out("x1T", [128, 8 * TALL])
            self.DMA("sp", o, self.xT[:].rearrange("p a n -> p (a n)"), self.dbg_sem, r=tkeys("xT", 0, TALL))
        P.phase_end()

    def phase_D2(self, b, l, last):
        P = self.P
        P.phase_begin()
        uid = P.uid
        K = lambda n: f"{n}_p{uid}"
        xm2 = [P.psbuf(f"xm{i}", [128, 8, 512], BF16) for i in range(2)]
        hid = P.psbuf("hid", [128, 32, 512], BF16)
        tt2 = [P.psbuf(f"tt{i}", [128, 512], F32) for i in range(2)]
        cnt = [0]
        cidx = [0]
        tmp = dict(sq=P.psbuf("lsq", [128, 512], F32), mean=P.psbuf("lmean", [128, 512], F32),
                   rstd=P.psbuf("lrstd", [128, 512], F32), t1=P.psbuf("lt1", [128, 512], F32))
        ring = self.make_ring("re", 2, 32 * 256)
        b1 = self.V(l, "b1", 0, 32)
        chunks = ([(0, 256, 1)] if not last else []) + [(256 + 512 * i, 512, 0) for i in range(4)]
        def do_ln2(c):
            if c is None:
                return
            m0, mln = c
            zs = [self.xT[:, fc, m0:m0 + mln] for fc in range(8)]
            xk = tkeys("xT", m0, mln)
            self.fm_layernorm(zs, [xk] * 8, zs, [xk] * 8, mln, [self.V(l, "ln2_g", i) for i in range(8)],
                              [self.V(l, "ln2_b", i) for i in range(8)], 1024.0, tmp)

        pending_ln = None
        for (n0, ln, who) in chunks:
            xm, xmk = xm2[cidx[0] % 2], K(f"xm{cidx[0] % 2}")
            cidx[0] += 1
            self.make_uT(xm, xmk, n0, ln, who, 3)
            for s in range(8):
                wt, wkey = self.ring_load(ring, l, "w1", s, 8 * 512)
                w3 = wt[:, 0:8 * 512].rearrange("p (k n) -> p k n", n=512)
                for q in range(4):
                    j = s * 4 + q
                    pi = j % 4
                    ps = self.ps[pi]
                    self.MM(ps[:, 0:ln], f"ps{pi}", [(w3[:, k, q * 128:(q + 1) * 128], xm[:, k, 0:ln]) for k in range(8)], [wkey, xmk])
                    bi_ = cnt[0] % 2
                    cnt[0] += 1
                    tt_, ttk = tt2[bi_], K(f"tt{bi_}")
                    self.TS("dve", tt_[:, 0:ln], ps[:, 0:ln], b1[:, j:j + 1], 0.0, ALU.add, ALU.max, [f"ps{pi}", f"vec{l}"], [ttk])
                    self.ACT(hid[:, j, 0:ln], tt_[:, 0:ln], AF.Square, [ttk], [K("hid")])
            do_ln2(pending_ln)
            pending_ln = None
            for s in range(4):
                wt, wkey = self.ring_load(ring, l, "w2", s, 32 * 256)
                w3 = wt[:].rearrange("p (k n) -> p k n", n=256)
                for q in range(2):
                    fc = s * 2 + q
                    pi = 4 + fc % 2
                    ps = self.ps[pi]
                    self.MM(ps[:, 0:ln], f"ps{pi}", [(w3[:, j, q * 128:(q + 1) * 128], hid[:, j, 0:ln]) for j in range(32)], [wkey, K("hid")])
                    wc = self.cur_b if who == 0 else 2
                    g2 = self.modT[:, 40 + fc, wc:wc + 1]
                    bg2 = self.modP[:, 24 + fc, wc:wc + 1]
                    bi_ = cnt[0] % 2
                    cnt[0] += 1
                    tt_, ttk = tt2[bi_], K(f"tt{bi_}")
                    self.TS("dve", tt_[:, 0:ln], ps[:, 0:ln], g2, bg2, ALU.mult, ALU.add, [f"ps{pi}"] + list(self.mk), [ttk])
                    xs = self.xT[:, fc, n0:n0 + ln]
                    self.STT(xs, xs, ALPHA, tt_[:, 0:ln], ALU.mult, ALU.add, tkeys("xT", n0, ln) + [ttk], tkeys("xT", n0, ln))
            pending_ln = (n0, ln)
        do_ln2(pending_ln)
        P.phase_end()


def _small_consts(bld):
    P = bld.P
    bld.epsc = P.sbuf("epsc", [128, 1], F32)
    bld.onec = P.sbuf("onec", [128, 1], F32)
    bld.MEMSET("pool", bld.epsc[:], EPS, ["epsc"])
    bld.MEMSET("pool", bld.onec[:], 1.0, ["onec"])


_orig_consts = Builder.consts


def _consts(self):
    _orig_consts(self)
    _small_consts(self)


Builder.consts = _consts


def rope_tables():
    pos = np.arange(NLAT)
    rr = (pos // 64).astype(np.float32)
    cc = (pos % 64).astype(np.float32)
    inv = (np.float32(10000.0) ** (-np.arange(8, dtype=np.float32) / np.float32(8))).astype(np.float32)
    C = np.zeros((32, NLAT), np.float32)
    S = np.zeros((32, NLAT), np.float32)
    for a, p in enumerate((rr, cc)):
        ang = (p[None, :] * inv[:, None]).astype(np.float32)
        for j in range(2):
            C[a * 16 + j * 8:a * 16 + j * 8 + 8] = np.cos(ang)
            S[a * 16 + j * 8:a * 16 + j * 8 + 8] = np.sin(ang) * (-1.0 if j == 0 else 1.0)
    return C, S


_NC_CACHE = {}


def kernel(**inputs):
    if "nc" not in _NC_CACHE:
        _NC_CACHE["nc"] = Builder().build()
    nc = _NC_CACHE["nc"]
    C, S = rope_tables()
    names = ["w_mod", "b_mod", "w_in", "b_in", "w_dw", "b_dw", "conv_norm_g", "conv_norm_b", "w_conv_out", "mlstm_norm_g",
             "w_mlstm_out", "q_norm_g", "w_uq", "kv_norm_g", "w_ukv", "w_mla_out", "w_out", "b_out", "ln1_g", "ln1_b",
             "w1", "b1", "w2", "b2", "ln2_g", "ln2_b", "c_ctx"]
    shared = {n: np.ascontiguousarray(np.asarray(inputs[n], dtype=np.float32)) for n in names}
    shared["rope_c"] = C
    shared["rope_s"] = S
    x = np.asarray(inputs["x"], dtype=np.float32)
    c = np.asarray(inputs["c"], dtype=np.float32)
    ctx = np.asarray(inputs["ctx"], dtype=np.float32)
    in_maps = []
    for i in range(8):
        m = dict(shared)
        m["x"] = np.ascontiguousarray(x[i * NB:(i + 1) * NB])
        m["c"] = np.ascontiguousarray(c[i * NB:(i + 1) * NB])
        m["ctx"] = np.ascontiguousarray(ctx[i * NB:(i + 1) * NB])
        in_maps.append(m)
    res = run_bass_kernel_spmd(nc, in_maps, core_ids=list(range(8)))
    return np.concatenate([np.asarray(r["out"], dtype=np.float32) for r in res.results], axis=0)
```
